# Optimizing a Trainium2 kernel written in Bass

```python
import functools
import jax, jax.numpy as jnp
from jax import lax
import numpy as np

D_MODEL = 2048
BATCH = 4
SEQ = 4096
DEPTH = 2

GRID_W = 64
CTX_LEN = 256
N_EVEN = (DEPTH + 1) // 2
N_ODD = DEPTH // 2
D_FF = 5504
N_MOD = 9
EPS = 1e-6
CHUNK = 128
Q_BLOCK = 128
ROPE_BASE = 10000.0
NEG_BIG = -1e30

RET_HEADS = 4
RET_DK = 128
RET_DV = 256
MLSTM_HEADS = 4
MLSTM_DK = 128
MLSTM_DV = 256
LRU_WIDTH = 1024
LRU_BLOCKS = 8
LRU_BLOCK = LRU_WIDTH // LRU_BLOCKS
LRU_CONV = 4
LRU_C = 8.0
MLA_HEADS = 8
MLA_Q_RANK = 512
MLA_KV_RANK = 256
MLA_NOPE = 128
MLA_ROPE = 64
MLA_V = 128

AB_SPLITS = (RET_HEADS * RET_DK, RET_HEADS * RET_DK, RET_HEADS * RET_DV, RET_HEADS * RET_DV,
             MLSTM_HEADS * MLSTM_DK, MLSTM_HEADS * MLSTM_DK, MLSTM_HEADS * MLSTM_DV, MLSTM_HEADS * MLSTM_DV,
             4 * MLSTM_HEADS)
AB_IN = sum(AB_SPLITS)
AB_OUT = RET_HEADS * RET_DV + MLSTM_HEADS * MLSTM_DV
CD_SPLITS = (LRU_WIDTH, LRU_WIDTH, MLA_Q_RANK, MLA_KV_RANK, MLA_ROPE)
CD_IN = sum(CD_SPLITS)
CD_OUT = LRU_WIDTH + MLA_HEADS * MLA_V

kernel_name = "hybrid_retention_mlstm_rglru_mla_dit"

F32 = jnp.float32


def split_cols(p, sizes):
    return jnp.split(p, np.cumsum(sizes)[:-1].tolist(), axis=-1)


def split_heads(t, n_heads):
    return t.reshape(t.shape[0], t.shape[1], n_heads, -1)


def rms_norm(x, g):
    xf = x.astype(F32)
    y = xf * lax.rsqrt(jnp.mean(xf * xf, axis=-1, keepdims=True) + EPS)
    return (y * g.astype(F32)).astype(x.dtype)


def head_layernorm(y, g):
    yf = y.astype(F32)
    mu = jnp.mean(yf, axis=-1, keepdims=True)
    var = jnp.mean(jnp.square(yf - mu), axis=-1, keepdims=True)
    yn = (yf - mu) * lax.rsqrt(var + EPS)
    return yn.reshape(y.shape[0], y.shape[1], -1) * g.astype(F32)


def modulate(h, shift, scale):
    return h * (1.0 + scale) + shift


def adaln(cond, w, b):
    return jnp.split(jax.nn.silu(cond) @ w + b, N_MOD, axis=-1)


def half_ffn(x, g, shift, scale, gate, wg, wu, wd):
    h = modulate(rms_norm(x, g), shift, scale)
    return x + 0.5 * gate * ((jax.nn.silu(h @ wg) * (h @ wu)) @ wd)


def rope_rotate(x, ang):
    m = ang.shape[-1]
    cos = jnp.cos(ang)[:, None, :].astype(x.dtype)
    sin = jnp.sin(ang)[:, None, :].astype(x.dtype)
    x1, x2 = x[..., :m], x[..., m:]
    return jnp.concatenate([x1 * cos - x2 * sin, x1 * sin + x2 * cos], axis=-1)


def grid_positions(n_tokens):
    rows = n_tokens // GRID_W
    row = jnp.repeat(jnp.arange(rows, dtype=F32), GRID_W)
    col = jnp.tile(jnp.arange(GRID_W, dtype=F32), rows)
    return row, col


def axial_rope(x):
    row, col = grid_positions(x.shape[1])
    half = MLA_ROPE // 2
    freqs = ROPE_BASE ** (-jnp.arange(half // 2, dtype=F32) / (half // 2))
    xn = x[..., :MLA_NOPE]
    xr = x[..., MLA_NOPE:MLA_NOPE + half]
    xc = x[..., MLA_NOPE + half:]
    return jnp.concatenate([xn, rope_rotate(xr, row[:, None] * freqs), rope_rotate(xc, col[:, None] * freqs)], axis=-1)


def retnet_angles(n_tokens):
    freqs = ROPE_BASE ** (-jnp.arange(RET_DK // 2, dtype=F32) / (RET_DK // 2))
    return jnp.arange(n_tokens, dtype=F32)[:, None] * freqs


def dwconv_centred(x, w, b):
    k = w.shape[0]
    out = lax.conv_general_dilated(x, w[:, None, :].astype(x.dtype), window_strides=(1,),
                                   padding=[(k // 2, k - 1 - k // 2)],
                                   dimension_numbers=('NWC', 'WIO', 'NWC'),
                                   feature_group_count=x.shape[-1])
    return out + b


def bidirectional(scan_fwd, scan_bwd, ctx_f, lat_f, ctx_b, lat_b, state0):
    flip = lambda t: jnp.flip(t, axis=1)
    yc_f, st_f = scan_fwd(ctx_f, state0)
    yl_f, _ = scan_fwd(lat_f, st_f)
    yc_b, st_b = scan_bwd(tuple(flip(t) for t in ctx_b), state0)
    yl_b, _ = scan_bwd(tuple(flip(t) for t in lat_b), st_b)
    return yl_f + flip(yl_b), yc_f + flip(yc_b)


def retention_chunkwise(seq, s0, log_gamma):
    q, k, v = seq
    B, T, H, dk = q.shape
    dv = v.shape[-1]
    n = T // CHUNK
    qc = q.reshape(B, n, CHUNK, H, dk)
    kc = k.reshape(B, n, CHUNK, H, dk)
    vc = v.reshape(B, n, CHUNK, H, dv)
    pos = jnp.arange(CHUNK, dtype=F32)
    rel = pos[:, None] - pos[None, :]
    decay = jnp.where((rel >= 0)[None], jnp.exp(jnp.maximum(rel, 0.0)[None] * log_gamma[:, None, None]), 0.0)
    scores = jnp.einsum('bnqhd,bnkhd->bnhqk', qc, kc) * decay
    inner = jnp.einsum('bnhqk,bnkhe->bnqhe', scores, vc)
    zeta = jnp.exp((CHUNK - 1.0 - pos)[:, None] * log_gamma[None, :])
    u = jnp.einsum('bnchd,bnche->bnhde', kc * zeta[:, :, None], vc).astype(F32)
    gamma_chunk = jnp.exp(CHUNK * log_gamma)[None, :, None, None]

    def step(s, u_j):
        return gamma_chunk * s + u_j, s

    s_last, s_prev = lax.scan(step, s0, jnp.moveaxis(u, 1, 0))
    s_prev = jnp.moveaxis(s_prev, 0, 1)
    xi = jnp.exp((pos + 1.0)[:, None] * log_gamma[None, :])
    cross = jnp.einsum('bnqhd,bnhde->bnqhe', qc, s_prev) * xi[:, :, None]
    return (inner + cross).reshape(B, T, H, dv), s_last


def mlstm_chunkwise(seq, state):
    q, k, v, i_pre, log_f = seq
    B, T, H, dk = q.shape
    dv = v.shape[-1]
    n = T // CHUNK
    qc = (q * dk ** -0.5).reshape(B, n, CHUNK, H, dk)
    kc = k.reshape(B, n, CHUNK, H, dk)
    vc = v.reshape(B, n, CHUNK, H, dv)
    ic = i_pre.reshape(B, n, CHUNK, H)
    b = jnp.cumsum(log_f.reshape(B, n, CHUNK, H), axis=2)
    lower = jnp.tril(jnp.ones((CHUNK, CHUNK), dtype=bool))
    log_d = jnp.where(lower[None, None, :, :, None],
                      b[:, :, :, None, :] - b[:, :, None, :, :] + ic[:, :, None, :, :], -jnp.inf)
    m_intra = jnp.max(log_d, axis=3)
    b_last = b[:, :, -1]
    log_w = b_last[:, :, None] - b + ic
    m_loc = jnp.max(log_w, axis=2)
    kw = kc * jnp.exp(log_w - m_loc[:, :, None])[..., None]
    u_c = jnp.einsum('bnchd,bnche->bnhde', kw, vc).astype(F32)
    u_n = jnp.sum(kw, axis=2).astype(F32)

    def step(carry, inp):
        c_s, n_s, m_s = carry
        uc, un, ml, bl = inp
        m_new = jnp.maximum(bl + m_s, ml)
        a = jnp.exp(bl + m_s - m_new)
        g = jnp.exp(ml - m_new)
        new = (a[..., None, None] * c_s + g[..., None, None] * uc, a[..., None] * n_s + g[..., None] * un, m_new)
        return new, carry

    final, prev = lax.scan(step, state, tuple(jnp.moveaxis(t, 1, 0) for t in (u_c, u_n, m_loc, b_last)))
    c_prev, n_prev, m_prev = (jnp.moveaxis(t, 0, 1) for t in prev)
    log_inter = b + m_prev[:, :, None]
    m_t = jnp.maximum(log_inter, m_intra)
    d = jnp.exp(log_d - m_t[:, :, :, None])
    s = jnp.einsum('bnqhd,bnkhd->bnqkh', qc, kc) * d
    inter = jnp.exp(log_inter - m_t)
    num = jnp.einsum('bnqkh,bnkhe->bnqhe', s, vc) + jnp.einsum('bnqhd,bnhde->bnqhe', qc, c_prev) * inter[..., None]
    den = jnp.sum(s, axis=3) + jnp.einsum('bnqhd,bnhd->bnqh', qc, n_prev) * inter
    h = num / jnp.maximum(jnp.abs(den), jnp.exp(-m_t))[..., None]
    return h.reshape(B, T, H, dv), final


def lru_scan(seq, h0):
    a, b = seq
    acc_a, acc_b = lax.associative_scan(lambda e1, e2: (e1[0] * e2[0], e2[0] * e1[1] + e2[1]), (a, b), axis=1)
    h = acc_a * h0[:, None, :] + acc_b
    return h, h[:, -1]


def lru_coeffs(xc, wa, ba, wx, bx, lam):
    xb = xc.reshape(xc.shape[0], xc.shape[1], LRU_BLOCKS, LRU_BLOCK)
    r = jax.nn.sigmoid(jnp.einsum('btgi,gij->btgj', xb, wa.astype(F32)).reshape(xc.shape) + ba.astype(F32))
    i = jax.nn.sigmoid(jnp.einsum('btgi,gij->btgj', xb, wx.astype(F32)).reshape(xc.shape) + bx.astype(F32))
    log_a = -LRU_C * r * jax.nn.softplus(-lam.astype(F32))
    a = jnp.exp(log_a)
    return a, jnp.sqrt(-jnp.expm1(2.0 * log_a)) * (i * xc)


def attend_blocks(q, k, v):
    B, S, H, d = q.shape
    nb = S // Q_BLOCK
    qb = jnp.moveaxis(q.reshape(B, nb, Q_BLOCK, H, d), 1, 0)
    kf = k.astype(F32)
    scale = d ** -0.5

    def one(q_blk):
        s = jnp.einsum('bqhd,bkhd->bhqk', q_blk.astype(F32), kf) * scale
        p = jax.nn.softmax(s, axis=-1)
        return jnp.einsum('bhqk,bkhe->bqhe', p.astype(v.dtype), v)

    o = lax.map(one, qb)
    return jnp.moveaxis(o, 0, 1).reshape(B, S, H, v.shape[-1])


def mixer_ab(h, hc, w_in, w_out, decay_logit, ret_gn_g, gate_b, mlstm_gn_g, need_ctx):
    dt = h.dtype
    B = h.shape[0]
    lat = split_cols(h @ w_in, AB_SPLITS)
    ctx = split_cols(hc @ w_in, AB_SPLITS)
    ang = retnet_angles(h.shape[1])
    log_gamma = jax.nn.log_sigmoid(decay_logit.astype(F32))

    def ret_seq(parts, rotate):
        q = split_heads(parts[0], RET_HEADS)
        k = split_heads(parts[1], RET_HEADS)
        v = split_heads(parts[2], RET_HEADS)
        if rotate:
            q, k = rope_rotate(q, ang), rope_rotate(k, ang)
        return (q, k * RET_DK ** -0.5, v)

    r_lat, r_ctx = ret_seq(lat, True), ret_seq(ctx, False)
    s0 = jnp.zeros((B, RET_HEADS, RET_DK, RET_DV), F32)
    ret_l, ret_c = bidirectional(functools.partial(retention_chunkwise, log_gamma=log_gamma[0]),
                                 functools.partial(retention_chunkwise, log_gamma=log_gamma[1]),
                                 r_ctx, r_lat, r_ctx, r_lat, s0)

    def mlstm_seqs(parts):
        q = split_heads(parts[4], MLSTM_HEADS)
        k = split_heads(parts[5], MLSTM_HEADS)
        v = split_heads(parts[6], MLSTM_HEADS)
        g = parts[8].astype(F32).reshape(q.shape[0], q.shape[1], 4, MLSTM_HEADS) + gate_b.astype(F32)
        fwd = (q, k, v, g[:, :, 0], jax.nn.log_sigmoid(g[:, :, 1]))
        bwd = (q, k, v, g[:, :, 2], jax.nn.log_sigmoid(g[:, :, 3]))
        return fwd, bwd

    ml_f, ml_b = mlstm_seqs(lat)
    mc_f, mc_b = mlstm_seqs(ctx)
    st0 = (jnp.zeros((B, MLSTM_HEADS, MLSTM_DK, MLSTM_DV), F32),
           jnp.zeros((B, MLSTM_HEADS, MLSTM_DK), F32),
           jnp.full((B, MLSTM_HEADS), NEG_BIG, F32))
    ml_l, ml_c = bidirectional(mlstm_chunkwise, mlstm_chunkwise, mc_f, ml_f, mc_b, ml_b, st0)

    def merge(ret, ml, parts):
        ret_y = jax.nn.silu(parts[3]) * head_layernorm(ret, ret_gn_g).astype(dt)
        ml_y = jax.nn.sigmoid(parts[7]) * head_layernorm(ml, mlstm_gn_g).astype(dt)
        return jnp.concatenate([ret_y, ml_y], axis=-1) @ w_out

    y = merge(ret_l, ml_l, lat)
    yc = merge(ret_c, ml_c, ctx) if need_ctx else None
    return y, yc


def mla_qkv(cq, ckv, kr, q_norm_g, kv_norm_g, w_uq, w_uk, w_uv, qk_norm_g, rotate):
    B, T, _ = cq.shape
    q = (rms_norm(cq, q_norm_g) @ w_uq).reshape(B, T, MLA_HEADS, MLA_NOPE + MLA_ROPE)
    ckv = rms_norm(ckv, kv_norm_g)
    k_nope = (ckv @ w_uk).reshape(B, T, MLA_HEADS, MLA_NOPE)
    v = (ckv @ w_uv).reshape(B, T, MLA_HEADS, MLA_V)
    k_rope = jnp.broadcast_to(kr[:, :, None, :], (B, T, MLA_HEADS, MLA_ROPE))
    k = jnp.concatenate([k_nope, k_rope], axis=-1)
    q = rms_norm(q, qk_norm_g[0])
    k = rms_norm(k, qk_norm_g[1])
    if rotate:
        q, k = axial_rope(q), axial_rope(k)
    return q, k, v


def mixer_cd(h, hc, w_in, w_out, conv_w, conv_b, wa, ba, wx, bx, lam,
             q_norm_g, kv_norm_g, w_uq, w_uk, w_uv, qk_norm_g, need_ctx):
    dt = h.dtype
    B, S, _ = h.shape
    yb, xb, cq, ckv, kr = split_cols(h @ w_in, CD_SPLITS)
    ybc, xbc, cqc, ckvc, krc = split_cols(hc @ w_in, CD_SPLITS)
    xl = dwconv_centred(xb, conv_w, conv_b).astype(F32)
    xcx = dwconv_centred(xbc, conv_w, conv_b).astype(F32)
    lat_f = lru_coeffs(xl, wa[0], ba[0], wx[0], bx[0], lam[0])
    lat_b = lru_coeffs(xl, wa[1], ba[1], wx[1], bx[1], lam[1])
    ctx_f = lru_coeffs(xcx, wa[0], ba[0], wx[0], bx[0], lam[0])
    ctx_b = lru_coeffs(xcx, wa[1], ba[1], wx[1], bx[1], lam[1])
    h0 = jnp.zeros((B, LRU_WIDTH), F32)
    rl, rc = bidirectional(lru_scan, lru_scan, ctx_f, lat_f, ctx_b, lat_b, h0)
    q, k, v = mla_qkv(cq, ckv, kr, q_norm_g, kv_norm_g, w_uq, w_uk, w_uv, qk_norm_g, True)
    qc, kc, vc = mla_qkv(cqc, ckvc, krc, q_norm_g, kv_norm_g, w_uq, w_uk, w_uv, qk_norm_g, False)
    att = attend_blocks(q, jnp.concatenate([kc, k], axis=1), jnp.concatenate([vc, v], axis=1))
    y = jnp.concatenate([jax.nn.gelu(yb) * rl.astype(dt), att.reshape(B, S, MLA_HEADS * MLA_V)], axis=-1) @ w_out
    yc = None
    if need_ctx:
        att_c = attend_blocks(qc, kc, vc)
        yc = jnp.concatenate([jax.nn.gelu(ybc) * rc.astype(dt),
                              att_c.reshape(B, hc.shape[1], MLA_HEADS * MLA_V)], axis=-1) @ w_out
    return y, yc


def setup_inputs(seed: int = 0) -> dict:
    key = jax.random.key(seed)
    keys = iter(jax.random.split(key, 40))

    def nrm(shape, std):
        return jax.random.normal(next(keys), shape, F32) * std

    D = D_MODEL
    x = nrm((BATCH, SEQ, D), 1.0)
    c = nrm((BATCH, D), 1.0)
    ctx = nrm((BATCH, CTX_LEN, D), 1.0)
    c_ctx = nrm((D,), 1.0)
    ada_w = nrm((DEPTH, D, N_MOD * D), 0.5 * D ** -0.5)
    ada_b = nrm((DEPTH, N_MOD * D), 0.01)
    norm_g = 1.0 + nrm((DEPTH, 3, D), 0.01)
    ffn_wg = nrm((DEPTH, 2, D, D_FF), D ** -0.5)
    ffn_wu = nrm((DEPTH, 2, D, D_FF), D ** -0.5)
    ffn_wd = nrm((DEPTH, 2, D_FF, D), D_FF ** -0.5)
    ab_w_in = nrm((N_EVEN, D, AB_IN), D ** -0.5)
    ab_w_out = nrm((N_EVEN, AB_OUT, D), AB_OUT ** -0.5)
    gamma = 1.0 - 2.0 ** (-jnp.linspace(5.0, 12.0, RET_HEADS))
    ret_decay_logit = (jnp.log(gamma) - jnp.log1p(-gamma)) + nrm((N_EVEN, 2, RET_HEADS), 0.1)
    ret_gn_g = 1.0 + nrm((N_EVEN, RET_HEADS * RET_DV), 0.01)
    f_bias = jnp.linspace(3.0, 6.0, MLSTM_HEADS)
    mlstm_gate_b = jnp.stack([nrm((N_EVEN, MLSTM_HEADS), 0.1), f_bias + nrm((N_EVEN, MLSTM_HEADS), 0.1),
                              nrm((N_EVEN, MLSTM_HEADS), 0.1), f_bias + nrm((N_EVEN, MLSTM_HEADS), 0.1)], axis=1)
    mlstm_gn_g = 1.0 + nrm((N_EVEN, MLSTM_HEADS * MLSTM_DV), 0.01)
    cd_w_in = nrm((N_ODD, D, CD_IN), D ** -0.5)
    cd_w_out = nrm((N_ODD, CD_OUT, D), CD_OUT ** -0.5)
    lru_conv_w = nrm((N_ODD, LRU_CONV, LRU_WIDTH), LRU_CONV ** -0.5)
    lru_conv_b = nrm((N_ODD, LRU_WIDTH), 0.01)
    lru_wa = nrm((N_ODD, 2, LRU_BLOCKS, LRU_BLOCK, LRU_BLOCK), LRU_BLOCK ** -0.5)
    lru_ba = nrm((N_ODD, 2, LRU_WIDTH), 0.01)
    lru_wx = nrm((N_ODD, 2, LRU_BLOCKS, LRU_BLOCK, LRU_BLOCK), LRU_BLOCK ** -0.5)
    lru_bx = nrm((N_ODD, 2, LRU_WIDTH), 0.01)
    a_init = jax.random.uniform(next(keys), (N_ODD, 2, LRU_WIDTH), F32, 0.9, 0.999) ** (1.0 / LRU_C)
    lru_lambda = jnp.log(a_init) - jnp.log1p(-a_init)
    mla_q_norm_g = 1.0 + nrm((N_ODD, MLA_Q_RANK), 0.01)
    mla_kv_norm_g = 1.0 + nrm((N_ODD, MLA_KV_RANK), 0.01)
    mla_w_uq = nrm((N_ODD, MLA_Q_RANK, MLA_HEADS * (MLA_NOPE + MLA_ROPE)), MLA_Q_RANK ** -0.5)
    mla_w_uk = nrm((N_ODD, MLA_KV_RANK, MLA_HEADS * MLA_NOPE), MLA_KV_RANK ** -0.5)
    mla_w_uv = nrm((N_ODD, MLA_KV_RANK, MLA_HEADS * MLA_V), MLA_KV_RANK ** -0.5)
    mla_qk_norm_g = 1.0 + nrm((N_ODD, 2, MLA_NOPE + MLA_ROPE), 0.01)
    return {"x": x, "c": c, "ctx": ctx, "c_ctx": c_ctx, "ada_w": ada_w, "ada_b": ada_b, "norm_g": norm_g,
            "ffn_wg": ffn_wg, "ffn_wu": ffn_wu, "ffn_wd": ffn_wd, "ab_w_in": ab_w_in, "ab_w_out": ab_w_out,
            "ret_decay_logit": ret_decay_logit, "ret_gn_g": ret_gn_g, "mlstm_gate_b": mlstm_gate_b,
            "mlstm_gn_g": mlstm_gn_g, "cd_w_in": cd_w_in, "cd_w_out": cd_w_out, "lru_conv_w": lru_conv_w,
            "lru_conv_b": lru_conv_b, "lru_wa": lru_wa, "lru_ba": lru_ba, "lru_wx": lru_wx, "lru_bx": lru_bx,
            "lru_lambda": lru_lambda, "mla_q_norm_g": mla_q_norm_g, "mla_kv_norm_g": mla_kv_norm_g,
            "mla_w_uq": mla_w_uq, "mla_w_uk": mla_w_uk, "mla_w_uv": mla_w_uv, "mla_qk_norm_g": mla_qk_norm_g}


def reference(x, c, ctx, c_ctx, ada_w, ada_b, norm_g, ffn_wg, ffn_wu, ffn_wd, ab_w_in, ab_w_out,
              ret_decay_logit, ret_gn_g, mlstm_gate_b, mlstm_gn_g, cd_w_in, cd_w_out, lru_conv_w,
              lru_conv_b, lru_wa, lru_ba, lru_wx, lru_bx, lru_lambda, mla_q_norm_g, mla_kv_norm_g,
              mla_w_uq, mla_w_uk, mla_w_uv, mla_qk_norm_g):
    for l in range(DEPTH):
        need_ctx = l < DEPTH - 1
        ml = [m[:, None, :] for m in adaln(c, ada_w[l], ada_b[l])]
        mc = adaln(c_ctx, ada_w[l], ada_b[l])
        x = half_ffn(x, norm_g[l, 0], ml[0], ml[1], ml[2], ffn_wg[l, 0], ffn_wu[l, 0], ffn_wd[l, 0])
        ctx = half_ffn(ctx, norm_g[l, 0], mc[0], mc[1], mc[2], ffn_wg[l, 0], ffn_wu[l, 0], ffn_wd[l, 0])
        h = modulate(rms_norm(x, norm_g[l, 1]), ml[3], ml[4])
        hc = modulate(rms_norm(ctx, norm_g[l, 1]), mc[3], mc[4])
        j = l // 2
        if l % 2 == 0:
            y, yc = mixer_ab(h, hc, ab_w_in[j], ab_w_out[j], ret_decay_logit[j], ret_gn_g[j],
                             mlstm_gate_b[j], mlstm_gn_g[j], need_ctx)
        else:
            y, yc = mixer_cd(h, hc, cd_w_in[j], cd_w_out[j], lru_conv_w[j], lru_conv_b[j], lru_wa[j], lru_ba[j],
                             lru_wx[j], lru_bx[j], lru_lambda[j], mla_q_norm_g[j], mla_kv_norm_g[j],
                             mla_w_uq[j], mla_w_uk[j], mla_w_uv[j], mla_qk_norm_g[j], need_ctx)
        x = x + ml[5] * y
        x = half_ffn(x, norm_g[l, 2], ml[6], ml[7], ml[8], ffn_wg[l, 1], ffn_wu[l, 1], ffn_wd[l, 1])
        if need_ctx:
            ctx = ctx + mc[5] * yc
            ctx = half_ffn(ctx, norm_g[l, 2], mc[6], mc[7], mc[8], ffn_wg[l, 1], ffn_wu[l, 1], ffn_wd[l, 1])
    return x
```

```python
import contextlib
import numpy as np
import concourse.bass as bass
import concourse.mybir as mybir
from concourse.bass_utils import run_bass_kernel_spmd

F32 = mybir.dt.float32
BF16 = mybir.dt.bfloat16
AF = mybir.ActivationFunctionType
ALU = mybir.AluOpType
AX = mybir.AxisListType


class Obj:
    __slots__ = ("name", "last_w", "readers", "sem", "dcount")

    def __init__(self, name):
        self.name = name
        self.last_w = None
        self.readers = []
        self.sem = None
        self.dcount = 0


class Op:
    __slots__ = ("stream", "kind", "fn", "deps", "sig", "idx", "holder", "cum", "n")

    def __init__(self, stream, kind, fn):
        self.stream = stream
        self.kind = kind
        self.fn = fn
        self.deps = []
        self.sig = False
        self.idx = None
        self.holder = None
        self.cum = None
        self.n = 1


class Prog:
    STREAMS = ("pe", "act", "dve", "pool", "sp")

    def __init__(self, nc):
        self.nc = nc
        self.ops = []
        self.stack = contextlib.ExitStack()
        self.nobj = 0

    def sbuf(self, name, shape, dtype):
        self.nobj += 1
        return self.stack.enter_context(self.nc.sbuf_tensor(f"{name}_s{self.nobj}", list(shape), dtype))

    def psum(self, name, shape, dtype=F32):
        self.nobj += 1
        return self.stack.enter_context(self.nc.psum_tensor(f"{name}_p{self.nobj}", list(shape), dtype))

    def obj(self, name=None):
        self.nobj += 1
        return Obj(name or f"o{self.nobj}")

    def objs(self, n, name="o"):
        return [self.obj(f"{name}{i}") for i in range(n)]

    def _add(self, op, reads, writes):
        deps = []
        for o in reads:
            if o.last_w is not None:
                deps.append(o.last_w)
        for o in writes:
            if o.last_w is not None:
                deps.append(o.last_w)
            deps.extend(o.readers)
        seen = set()
        for d in deps:
            if d is op or id(d) in seen:
                continue
            seen.add(id(d))
            op.deps.append(d)
        raw = set()
        for o in reads:
            if o.last_w is not None:
                raw.add(id(o.last_w))
        pr = []
        for d in op.deps:
            if d.kind == "c" and op.kind == "c" and d.stream == op.stream:
                if op.stream == "pe":
                    continue
                if id(d) not in raw:
                    continue
            pr.append(d)
        op.deps = pr
        for d in op.deps:
            d.sig = True
        for o in reads:
            o.readers.append(op)
        for o in writes:
            o.last_w = op
            o.readers = []
        self.ops.append(op)
        return op

    def op(self, stream, fn, reads=(), writes=()):
        return self._add(Op(stream, "c", fn), list(reads), list(writes))

    def dma(self, stream, fn, reads=(), writes=(), holder=None, n=1):
        op = Op(stream, "d", fn)
        op.n = n
        op.holder = holder if holder is not None else (list(writes)[0] if writes else list(reads)[0])
        op.sig = True
        return self._add(op, list(reads), list(writes))

    def emit(self):
        nc = self.nc
        st = self.stack
        sems = {s: st.enter_context(nc.semaphore("sem_" + s)) for s in ("pe", "act", "dve", "pool")}
        cnt = {s: 0 for s in self.STREAMS}
        for op in self.ops:
            if op.kind == "c":
                if op.sig:
                    cnt[op.stream] += 1
                    op.idx = cnt[op.stream]
            else:
                h = op.holder
                if h.sem is None:
                    h.sem = st.enter_context(nc.semaphore("dsem_" + h.name))
                h.dcount += 16 * op.n
                op.cum = h.dcount
        per = {s: [o for o in self.ops if o.stream == s] for s in self.STREAMS}
        holders = {}
        for op in self.ops:
            if op.kind == "d":
                holders[id(op.holder)] = op.holder

        def run_stream(s, eng, final=False):
            waited = {}
            for op in per[s]:
                for d in op.deps:
                    if d.kind == "c":
                        sem, val = sems[d.stream], d.idx
                    else:
                        sem, val = d.holder.sem, d.cum
                    k = id(sem)
                    if waited.get(k, 0) >= val:
                        continue
                    waited[k] = val
                    eng.wait_ge(sem, val)
                r = op.fn(eng)
                if op.kind == "c":
                    if op.sig:
                        r.then_inc(sems[s], 1)
                else:
                    rs = r if isinstance(r, (list, tuple)) else [r]
                    assert len(rs) == op.n, (len(rs), op.n)
                    for ins in rs:
                        ins.then_inc(op.holder.sem, 16)
            if final:
                for h in holders.values():
                    if waited.get(id(h.sem), 0) < h.dcount:
                        eng.wait_ge(h.sem, h.dcount)
                for cs in ("pe", "act", "dve", "pool"):
                    if cnt[cs] > 0:
                        eng.wait_ge(sems[cs], cnt[cs])

        with nc.Block() as block:
            @block.tensor
            def _(e):
                run_stream("pe", e)

            @block.scalar
            def _(e):
                run_stream("act", e)

            @block.vector
            def _(e):
                run_stream("dve", e)

            @block.gpsimd
            def _(e):
                run_stream("pool", e)

            @block.sync
            def _(e):
                run_stream("sp", e, final=True)
        self.stack.close()


D = 2048
DFF = 5504
KC = D // 128
FC = DFF // 128
EPS = 1e-6


class PsumPool:
    def __init__(self, P, n=8):
        self.t = [P.psum(f"ps{i}", [128, 512], F32) for i in range(n)]
        self.o = P.objs(n, "ps")
        self.i = 0
        self.n = n

    def next(self):
        i = self.i
        self.i = (self.i + 1) % self.n
        return self.t[i], self.o[i]


def mm(P, out_ap, out_o, lhsT, lhs_o, rhs, rhs_o, start, stop):
    rd = [o for o in (lhs_o if isinstance(lhs_o, (list, tuple)) else [lhs_o])]
    rd += [o for o in (rhs_o if isinstance(rhs_o, (list, tuple)) else [rhs_o])]
    return P.op("pe", lambda e: e.matmul(out_ap, lhsT, rhs, start=start, stop=stop), reads=rd, writes=[out_o])


def ffn_consts(P, modv_d, gn_d):
    c = {}
    c["ones"] = P.sbuf("ones_bf", [128, 128], BF16)
    c["ones_o"] = P.obj("ones")
    P.op("pool", lambda e: e.memset(c["ones"][:, :], 1.0), writes=[c["ones_o"]])
    c["mod"] = P.sbuf("modsb", [128, 6, 16], F32)
    c["gn"] = P.sbuf("gnsb", [128, 16], F32)
    c["A"] = P.sbuf("Asb", [128, 2, 16], F32)
    c["G"] = P.sbuf("Gsb", [128, 2, 16], F32)
    mo, go, c["A_o"], c["G_o"] = P.objs(4, "modc")
    c["mod_o"] = mo
    P.dma("sp", lambda e: e.dma_start(out=c["mod"][:, :, :], in_=modv_d.rearrange("p (m k) -> p m k", m=6)), writes=[mo])
    P.dma("sp", lambda e: e.dma_start(out=c["gn"][:, :], in_=gn_d), writes=[go])
    for g in range(2):
        P.op("dve", lambda e, g=g: e.tensor_scalar(out=c["A"][:, g, :], in0=c["mod"][:, 3 * g + 1, :], scalar1=1.0, scalar2=None, op0=ALU.add),
             reads=[mo], writes=[c["A_o"]])
        P.op("dve", lambda e, g=g: e.tensor_tensor(out=c["A"][:, g, :], in0=c["A"][:, g, :], in1=c["gn"][:, :], op=ALU.mult),
             reads=[c["A_o"], go], writes=[c["A_o"]])
        P.op("dve", lambda e, g=g: e.tensor_scalar(out=c["G"][:, g, :], in0=c["mod"][:, 3 * g + 2, :], scalar1=0.5, scalar2=None, op0=ALU.mult),
             reads=[mo], writes=[c["G_o"]])
    return c


def norm_mod(P, pp, c, xT_d, t0, n, g, h, h_o, hoff, bufs, src_objs=()):
    xp, xp_o, sq, sq_o, tmp, tmp_o, rs, rs_o = bufs
    PIECE = 128
    xv = xT_d.rearrange("(k p) t -> p k t", p=128)
    for pi, p0 in enumerate(range(0, n, PIECE)):
        pn = min(PIECE, n - p0)
        b = norm_mod.cnt % 2
        norm_mod.cnt += 1
        P.dma("sp", lambda e, b=b, p0=p0, pn=pn: [
            e.dma_start(out=xp[b][:, q * 4:(q + 1) * 4, :pn], in_=xv[:, q * 4:(q + 1) * 4, t0 + p0:t0 + p0 + pn]) for q in range(4)],
            reads=list(src_objs), writes=[xp_o[b]], n=4)
        P.op("act", lambda e, b=b, pn=pn: e.activation(out=sq[b][:, :, :pn], in_=xp[b][:, :, :pn], func=AF.Square),
             reads=[xp_o[b]], writes=[sq_o[b]])
        ps, ps_o = pp.next()
        for kc in range(KC):
            mm(P, ps[:, :pn], ps_o, c["ones"][:, :], c["ones_o"], sq[b][:, kc, :pn], sq_o[b], kc == 0, kc == KC - 1)
        P.op("act", lambda e, b=b, pn=pn, ps=ps: e.activation(out=rs[b][:, :pn], in_=ps[:, :pn], func=AF.Sqrt, bias=c["eps"][:, 0:1], scale=1.0 / D),
             reads=[ps_o, c["eps_o"]], writes=[rs_o[b]])
        P.op("dve", lambda e, b=b, pn=pn: e.reciprocal(out=rs[b][:, :pn], in_=rs[b][:, :pn]), reads=[rs_o[b]], writes=[rs_o[b]])
        for kc in range(KC):
            tb = norm_mod.tcnt % 4
            norm_mod.tcnt += 1
            P.op("dve", lambda e, b=b, pn=pn, kc=kc, tb=tb: e.tensor_tensor(out=tmp[:, tb, :pn], in0=xp[b][:, kc, :pn], in1=rs[b][:, :pn], op=ALU.mult),
                 reads=[xp_o[b], rs_o[b]], writes=[tmp_o[tb]])
            P.op("act", lambda e, pn=pn, kc=kc, tb=tb, p0=p0: e.activation(
                out=h[:, kc, hoff + p0:hoff + p0 + pn], in_=tmp[:, tb, :pn], func=AF.Identity,
                bias=c["mod"][:, 3 * g, kc:kc + 1], scale=c["A"][:, g, kc:kc + 1]),
                reads=[tmp_o[tb], c["A_o"], c["mod_o"]], writes=[h_o])


norm_mod.cnt = 0
norm_mod.tcnt = 0


def norm_bufs(P):
    xp = [P.sbuf(f"xp{i}", [128, KC, 128], F32) for i in range(2)]
    sq = [P.sbuf(f"sq{i}", [128, KC, 128], BF16) for i in range(2)]
    tmp = P.sbuf("ntmp", [128, 4, 128], F32)
    rs = [P.sbuf(f"rs{i}", [128, 128], F32) for i in range(2)]
    return (xp, P.objs(2, "xp"), sq, P.objs(2, "sq"), tmp, P.objs(4, "ntmp"), rs, P.objs(2, "rs"))


def add_eps(P, c):
    c["eps"] = P.sbuf("eps", [128, 1], F32)
    c["eps_o"] = P.obj("eps")
    P.op("pool", lambda e: e.memset(c["eps"][:, :], EPS), writes=[c["eps_o"]])


def build_ffn(T, blocks, combine=False):
    nc = bass.Bass("TRN2", target_bir_lowering=False)
    xT = nc.dram_tensor("xT", [D, T], F32, kind="ExternalInput").ap()
    modv = nc.dram_tensor("modv", [128, 96], F32, kind="ExternalInput").ap()
    gn = nc.dram_tensor("gn", [128, KC], F32, kind="ExternalInput").ap()
    wgu = nc.dram_tensor("wgu", [FC, 128, 2 * KC * 128], BF16, kind="ExternalInput").ap()
    wd = nc.dram_tensor("wd", [KC, 128, FC * 128], BF16, kind="ExternalInput").ap()
    yT = nc.dram_tensor("yT", [D, T], F32, kind="ExternalOutput").ap()
    P = Prog(nc)
    pp = PsumPool(P)
    c = ffn_consts(P, modv, gn)
    add_eps(P, c)
    nb = norm_bufs(P)
    PIECES = [(0, 1024), (1024, 1024)] + ([(2048, 128)] if T > 2048 else [])
    xsrc_o = None
    MAXB = max(sum(s[1] for s in blk) for blk in blocks)
    arena = P.sbuf("arena", [128, FC * MAXB // 2], F32)
    a2 = arena[:, :].bitcast(BF16)
    alias_objs = []
    xT_in = xT
    if combine:
        y0T = nc.dram_tensor("y0T", [D, T], F32, kind="ExternalInput").ap()
        y1T = nc.dram_tensor("y1T", [D, T], F32, kind="ExternalInput").ap()
        g5d = nc.dram_tensor("g5", [128, 2 * KC], F32, kind="ExternalInput").ap()
        xc_d = nc.dram_tensor("xc_scr", [D, T], F32).ap()
        g5 = P.sbuf("g5sb", [128, 2, KC], F32); g5_o = P.obj("g5")
        P.dma("sp", lambda e: e.dma_start(out=g5[:, :, :], in_=g5d.rearrange("p (a k) -> p a k", a=2)), writes=[g5_o])
        cx = [arena[:, i * 1024:(i + 1) * 1024] for i in range(2)]; cx_o = P.objs(2, "cx")
        cy0 = [arena[:, (2 + i) * 1024:(3 + i) * 1024] for i in range(2)]; cy0_o = P.objs(2, "cy0")
        cy1 = [arena[:, (4 + i) * 1024:(5 + i) * 1024] for i in range(2)]; cy1_o = P.objs(2, "cy1")
        alias_objs = cx_o + cy0_o + cy1_o
        xsrc_o = [[P.obj(f"xc{dc}_{pi}") for pi in range(len(PIECES))] for dc in range(KC)]
        ci = 0
        for dc in range(KC):
            rows = slice(dc * 128, (dc + 1) * 128)
            for pi, (c0, n) in enumerate(PIECES):
                b = ci % 2; ci += 1
                g = 0 if c0 < 2048 else 1
                P.dma("sp", lambda e, b=b, rows=rows, c0=c0, n=n: e.dma_start(out=cx[b][:, :n], in_=xT_in[rows, c0:c0 + n]), writes=[cx_o[b]])
                P.dma("sp", lambda e, b=b, rows=rows, c0=c0, n=n: e.dma_start(out=cy0[b][:, :n], in_=y0T[rows, c0:c0 + n]), writes=[cy0_o[b]])
                P.dma("sp", lambda e, b=b, rows=rows, c0=c0, n=n: e.dma_start(out=cy1[b][:, :n], in_=y1T[rows, c0:c0 + n]), writes=[cy1_o[b]])
                P.op("pool", lambda e, b=b, n=n: e.tensor_tensor(out=cy0[b][:, :n], in0=cy0[b][:, :n], in1=cy1[b][:, :n], op=ALU.add),
                     reads=[cy0_o[b], cy1_o[b]], writes=[cy0_o[b]])
                P.op("dve", lambda e, b=b, n=n, g=g, dc=dc: e.scalar_tensor_tensor(out=cx[b][:, :n], in0=cy0[b][:, :n], scalar=g5[:, g, dc:dc + 1], in1=cx[b][:, :n], op0=ALU.mult, op1=ALU.add),
                     reads=[cy0_o[b], cx_o[b], g5_o], writes=[cx_o[b]])
                P.dma("pool", lambda e, b=b, rows=rows, c0=c0, n=n: e.dma_start(out=xc_d[rows, c0:c0 + n], in_=cx[b][:, :n]), reads=[cx_o[b]], writes=[xsrc_o[dc][pi]], holder=cx_o[b])
        xT = xc_d

    def src_for(t0, n, dcs):
        if xsrc_o is None:
            return []
        out = []
        for pi, (c0, pn) in enumerate(PIECES):
            if t0 < c0 + pn and t0 + n > c0:
                out += [xsrc_o[dc][pi] for dc in dcs]
        return out
    h = P.sbuf("h", [128, KC, MAXB], BF16)
    NW = 3
    wbuf = [P.sbuf(f"wgu{i}", [128, 2 * KC * 128], BF16) for i in range(NW)]
    wbuf_o = P.objs(NW, "wgu")
    wdbuf = [P.sbuf(f"wd{i}", [128, FC * 128], BF16) for i in range(2)]
    wdbuf_o = P.objs(2, "wd")
    xs = [P.sbuf(f"xs{i}", [128, MAXB], F32) for i in range(2)]
    xs_o = P.objs(2, "xs")
    osb = [P.sbuf(f"os{i}", [128, MAXB], F32) for i in range(2)]
    os_o = P.objs(2, "os")
    sg = [P.sbuf(f"sg{i}", [128, 512], F32) for i in range(2)]
    sg_o = P.objs(2, "sg")
    wcnt = 0
    dcnt = 0
    sgc = 0
    for blk in blocks:
        offs = []
        o = 0
        for (t0, n, g) in blk:
            offs.append(o)
            o += n
        h_o = [P.obj("h") for _ in blk]
        a_o = [[P.obj("a") for _ in blk] for _ in range(FC)]
        for si, (t0, n, g) in enumerate(blk):
            norm_mod(P, pp, c, xT, t0, n, g, h, h_o[si], offs[si], nb, src_for(t0, n, range(KC)))
        for fc in range(FC):
            wb = wcnt % NW
            wcnt += 1
            P.dma("sp", lambda e, wb=wb, fc=fc: e.dma_start(out=wbuf[wb][:, :], in_=wgu[fc]), writes=[wbuf_o[wb]])
            for si, (t0, n, g) in enumerate(blk):
                gps, gps_o = pp.next()
                ups, ups_o = pp.next()
                for kc in range(KC):
                    mm(P, gps[:, :n], gps_o, wbuf[wb][:, kc * 128:(kc + 1) * 128], wbuf_o[wb],
                       h[:, kc, offs[si]:offs[si] + n], h_o[si], kc == 0, kc == KC - 1)
                for kc in range(KC):
                    mm(P, ups[:, :n], ups_o, wbuf[wb][:, (KC + kc) * 128:(KC + kc + 1) * 128], wbuf_o[wb],
                       h[:, kc, offs[si]:offs[si] + n], h_o[si], kc == 0, kc == KC - 1)
                sb = sgc % 2
                sgc += 1
                P.op("act", lambda e, sb=sb, n=n, gps=gps: e.activation(out=sg[sb][:, :n], in_=gps[:, :n], func=AF.Silu),
                     reads=[gps_o], writes=[sg_o[sb]])
                P.op("dve", lambda e, sb=sb, n=n, ups=ups, fc=fc, off=offs[si]: e.tensor_tensor(
                    out=a2[:, fc * MAXB + off:fc * MAXB + off + n], in0=ups[:, :n], in1=sg[sb][:, :n], op=ALU.mult),
                    reads=[ups_o, sg_o[sb]], writes=[a_o[fc][si]] + alias_objs)
                alias_objs = []
        for dc in range(KC):
            db = dcnt % 2
            dcnt += 1
            P.dma("sp", lambda e, db=db, dc=dc: e.dma_start(out=wdbuf[db][:, :], in_=wd[dc]), writes=[wdbuf_o[db]])
            P.dma("sp", lambda e, db=db, dc=dc, blk=blk, offs=offs: [
                e.dma_start(out=xs[db][:, offs[si]:offs[si] + n], in_=xT[dc * 128:(dc + 1) * 128, t0:t0 + n])
                for si, (t0, n, g) in enumerate(blk)], reads=[o_ for (t0, n, g) in blk for o_ in src_for(t0, n, [dc])], writes=[xs_o[db]], n=len(blk))
            yps = [pp.next() for _ in blk]
            for fc in range(FC):
                for si, (t0, n, g) in enumerate(blk):
                    mm(P, yps[si][0][:, :n], yps[si][1], wdbuf[db][:, fc * 128:(fc + 1) * 128], wdbuf_o[db],
                       a2[:, fc * MAXB + offs[si]:fc * MAXB + offs[si] + n], a_o[fc][si], fc == 0, fc == FC - 1)
            for si, (t0, n, g) in enumerate(blk):
                P.op("dve", lambda e, db=db, n=n, g=g, dc=dc, off=offs[si], y=yps[si][0]: e.scalar_tensor_tensor(
                    out=osb[db][:, off:off + n], in0=y[:, :n], scalar=c["G"][:, g, dc:dc + 1], in1=xs[db][:, off:off + n],
                    op0=ALU.mult, op1=ALU.add), reads=[yps[si][1], c["G_o"], xs_o[db]], writes=[os_o[db]])
            P.dma("pool", lambda e, db=db, dc=dc, blk=blk, offs=offs: [
                e.dma_start(out=yT[dc * 128:(dc + 1) * 128, t0:t0 + n], in_=osb[db][:, offs[si]:offs[si] + n])
                for si, (t0, n, g) in enumerate(blk)], reads=[os_o[db]], n=len(blk))
    P.emit()
    return nc


FFN_BLOCKS = [[(0, 512, 0), (512, 256, 0)], [(768, 512, 0), (1280, 256, 0)], [(1536, 512, 0), (2048, 128, 1)]]


ADA_N = 18432 // 8
ADA_CH = 384


def build_prep(ncols, chunk=4096):
    nc = bass.Bass("TRN2", target_bir_lowering=False)
    x = nc.dram_tensor("wf32", [128, ncols], F32, kind="ExternalInput").ap()
    y = nc.dram_tensor("wbf16", [128, ncols], BF16, kind="ExternalOutput").ap()
    condT = nc.dram_tensor("condT", [128, KC * 5], F32, kind="ExternalInput").ap()
    adaw = nc.dram_tensor("adaw", [2, D, ADA_N], F32, kind="ExternalInput").ap()
    adab = nc.dram_tensor("adab", [2, 5, ADA_N], F32, kind="ExternalInput").ap()
    adao = nc.dram_tensor("adao", [2, 5, ADA_N], F32, kind="ExternalOutput").ap()
    P = Prog(nc)
    pp = PsumPool(P)
    ct = P.sbuf("ct", [128, KC, 5], F32); ct_o = P.obj("ct")
    sc = P.sbuf("sc", [128, KC, 5], F32); sc_o = P.obj("sc")
    P.dma("sp", lambda e: e.dma_start(out=ct[:, :, :], in_=condT.rearrange("p (k r) -> p k r", r=5)), writes=[ct_o])
    P.op("act", lambda e: e.activation(out=sc[:, :, :], in_=ct[:, :, :], func=AF.Silu), reads=[ct_o], writes=[sc_o])
    wsl = [P.sbuf(f"adaw{i}", [128, KC, ADA_CH], F32) for i in range(2)]; wsl_o = P.objs(2, "adaw")
    bsb = [P.sbuf(f"adab{i}", [5, ADA_CH], F32) for i in range(2)]; bsb_o = P.objs(2, "adab")
    osb = [P.sbuf(f"adao{i}", [5, ADA_CH], F32) for i in range(2)]; osb_o = P.objs(2, "adao")
    i = 0
    for l in range(2):
        wv = adaw[l].rearrange("(k p) n -> p k n", p=128)
        for c0 in range(0, ADA_N, ADA_CH):
            b = i % 2; i += 1
            P.dma("sp", lambda e, b=b, c0=c0, wv=wv: [e.dma_start(out=wsl[b][:, q * 4:(q + 1) * 4, :], in_=wv[:, q * 4:(q + 1) * 4, c0:c0 + ADA_CH]) for q in range(4)],
                  writes=[wsl_o[b]], n=4)
            P.dma("sp", lambda e, b=b, c0=c0, l=l: e.dma_start(out=bsb[b][:, :], in_=adab[l, :, c0:c0 + ADA_CH]), writes=[bsb_o[b]])
            ps, ps_o = pp.next()
            for kc in range(KC):
                mm(P, ps[0:5, :ADA_CH], ps_o, sc[:, kc, :], sc_o, wsl[b][:, kc, :], wsl_o[b], kc == 0, kc == KC - 1)
            P.op("dve", lambda e, b=b, ps=ps: e.tensor_tensor(out=osb[b][:, :], in0=ps[0:5, :ADA_CH], in1=bsb[b][:, :], op=ALU.add),
                 reads=[ps_o, bsb_o[b]], writes=[osb_o[b]])
            P.dma("pool", lambda e, b=b, c0=c0, l=l: e.dma_start(out=adao[l, :, c0:c0 + ADA_CH], in_=osb[b][:, :]), reads=[osb_o[b]])
    NB = 3
    xin = [P.sbuf(f"xin{i}", [128, chunk], F32) for i in range(NB)]
    xo = [P.sbuf(f"xo{i}", [128, chunk], BF16) for i in range(NB)]
    oin = P.objs(NB, "xin"); oo = P.objs(NB, "xo")
    nch = ncols // chunk
    assert nch * chunk == ncols
    for c in range(nch):
        b = c % NB
        sl = slice(c * chunk, (c + 1) * chunk)
        P.dma("sp", lambda e, b=b, sl=sl: e.dma_start(out=xin[b][:, :], in_=x[:, sl]), writes=[oin[b]])
        eng = ("dve", "pool", "act")[c % 3]
        if eng == "act":
            P.op("act", lambda e, b=b: e.copy(out=xo[b][:, :], in_=xin[b][:, :]), reads=[oin[b]], writes=[oo[b]])
        else:
            P.op(eng, lambda e, b=b: e.tensor_copy(out=xo[b][:, :], in_=xin[b][:, :]), reads=[oin[b]], writes=[oo[b]])
        P.dma("sp", lambda e, b=b, sl=sl: e.dma_start(out=y[:, sl], in_=xo[b][:, :]), reads=[oo[b]])
    P.emit()
    return nc


TA = 4352
NCH = TA // 128
SLOTW = 776
NEG = -30000.0


def V(P, eng, fn, reads, writes, *a, **kw):
    return P.op(eng, lambda e: getattr(e, fn)(*a, **kw), reads=reads, writes=writes)


class Ring:
    def __init__(self, P, name, shape, dtype, n):
        self.t = [P.sbuf(f"{name}{i}", shape, dtype) for i in range(n)]
        self.o = P.objs(n, name)
        self.i = 0
        self.n = n

    def next(self):
        i = self.i
        self.i = (i + 1) % self.n
        return self.t[i], self.o[i]


class StopBuild(Exception):
    pass


def build_ab(stage=99, slots=(0, 1, 2, 3)):
    try:
        return _build_ab(stage, slots)
    except StopBuild as e:
        e.args[0].emit()
        return e.args[1]


def _build_ab(stage, slots):
    nc = bass.Bass("TRN2", target_bir_lowering=False)
    di = lambda n, s, d: nc.dram_tensor(n, s, d, kind="ExternalInput").ap()
    xT = di("xT", [D, TA], F32)
    modv = di("modv", [128, 96], F32)
    gn = di("gn", [128, KC], F32)
    win = di("win", [4, 128, KC * SLOTW], BF16)
    wout = di("wout", [KC, 128, 8 * 128], BF16)
    hng = di("hng", [128, 8], F32)
    gpar = di("gpar", [128, 4 * 4], F32)
    cst = di("cst", [128, 6 * 128], F32)
    ropec = di("ropec", [128, 4096], F32)
    ropes = di("ropes", [128, 4096], F32)
    yT = nc.dram_tensor("yT", [D, TA], F32, kind="ExternalOutput").ap()
    hT_d = nc.dram_tensor("hT_scr", [D, TA], BF16).ap()
    m_d = nc.dram_tensor("m_scr", [8, 128, TA], BF16).ap()
    P = Prog(nc)
    pp = PsumPool(P)
    c = ffn_consts(P, modv, gn)
    add_eps(P, c)
    nb = norm_bufs(P)
    cs = P.sbuf("cst_sb", [128, 6, 128], F32); cs_o = P.obj("cst")
    P.dma("sp", lambda e: e.dma_start(out=cs[:, :, :], in_=cst.rearrange("p (a b) -> p a b", a=6)), writes=[cs_o])
    TRI = [cs[:, 0, :], cs[:, 1, :]]
    NMASK = [cs[:, 2, :], cs[:, 3, :]]
    PERM = cs[:, 4, :]
    onesf = P.sbuf("onesf", [128, 128], F32); onesf_o = P.obj("onesf")
    V(P, "pool", "memset", [], [onesf_o], onesf[:, :], 1.0)
    one1 = P.sbuf("one1", [128, 1], F32); one1_o = P.obj("one1")
    V(P, "pool", "memset", [], [one1_o], one1[:, :], 1.0)
    identb = P.sbuf("identb", [128, 128], BF16); identb_o = P.obj("identb")
    V(P, "dve", "tensor_copy", [cs_o], [identb_o], out=identb[:, :], in_=cs[:, 5, :])
    hg = P.sbuf("hng_sb", [128, 8], F32); hg_o = P.obj("hng")
    P.dma("sp", lambda e: e.dma_start(out=hg[:, :], in_=hng), writes=[hg_o])
    gp = P.sbuf("gpar_sb", [128, 4, 4], F32); gp_o = P.obj("gpar")
    P.dma("sp", lambda e: e.dma_start(out=gp[:, :, :], in_=gpar.rearrange("p (a b) -> p a b", a=4)), writes=[gp_o])

    hst = Ring(P, "hst", [128, KC, 256], BF16, 2)
    hTd_o = P.objs(TA // 256, "hTd")
    hv = hT_d.rearrange("(k p) t -> p k t", p=128)
    for t0 in range(0, TA, 256):
        g = 1 if t0 < 256 else 0
        ht, ht_o = hst.next()
        norm_mod(P, pp, c, xT, t0, 256, g, ht, ht_o, 0, nb)
        P.dma("pool", lambda e, ht=ht, t0=t0: e.dma_start(out=hv[:, :, t0:t0 + 256], in_=ht[:, :, :]), reads=[ht_o], writes=[hTd_o[t0 // 256]], holder=ht_o)

    if stage == 0:
        raise StopBuild(P, nc)
    wsl = P.sbuf("wsl", [128, KC, SLOTW], BF16); wsl_o = P.obj("wsl")
    qT = P.sbuf("qT", [128, TA], BF16); qT_o = P.objs(TA // 256, "qT")
    kT = P.sbuf("kT", [128, TA], BF16); kT_o = P.objs(TA // 256, "kT")
    ktm = P.sbuf("ktm", [128, NCH, 128], BF16); ktm_o = P.obj("ktm")
    vtm = P.sbuf("vtm", [128, NCH, 256], BF16); vtm_o = P.obj("vtm")
    gt = P.sbuf("gt", [128, NCH, 4], F32); gt_o = P.obj("gt")
    hacc = P.sbuf("hacc", [128, 2, TA], F32); hacc_o = P.objs(NCH, "hacc")
    ipre = P.sbuf("ipre", [128, 2 * NCH], F32)
    logf = P.sbuf("logf", [128, 2 * NCH], F32)
    btm = P.sbuf("btm", [128, 2 * NCH], F32)
    ekey = P.sbuf("ekey", [128, 2 * NCH], F32)
    wst = P.sbuf("wst", [128, 2 * NCH], F32)
    EB = P.sbuf("EB", [128, 2 * NCH], F32)
    gts_o = P.obj("gts")
    lg2 = P.sbuf("lg2", [128, 2], F32)
    St = P.sbuf("St", [128, 384], F32); St_o = P.obj("St")
    Stb = P.sbuf("Stb", [128, 384], BF16); Stb_o = P.obj("Stb")
    qf = Ring(P, "qf", [128, 256], F32, 2)
    rc_ = Ring(P, "rc", [128, 256], F32, 2)
    rs_ = Ring(P, "rs", [128, 256], F32, 2)
    t1 = Ring(P, "t1", [128, 256], F32, 2)
    t2 = Ring(P, "t2", [128, 256], F32, 2)
    lrow = Ring(P, "lrow", [128, 128], F32, 2)
    brm = Ring(P, "brm", [128, 128], F32, 2)
    Wt = Ring(P, "Wt", [128, 128], F32, 2)
    Eb = Ring(P, "Eb", [128, 128], F32, 2)
    PT = Ring(P, "PT", [128, 128], BF16, 2)
    qs = Ring(P, "qs", [128, 128], BF16, 2)
    kw = Ring(P, "kw", [128, 128], BF16, 2)
    dd = Ring(P, "dd", [128, 128], F32, 2)
    hn = Ring(P, "hn", [128, 128], F32, 2)
    onesb = c["ones"]; onesb_o = c["ones_o"]
    sqf = Ring(P, "sqf", [128, 2, 512], F32, 1)
    mean = Ring(P, "mean", [128, 512], F32, 1)
    rstd = Ring(P, "rstd", [128, 512], F32, 1)
    ogb = Ring(P, "ogb", [128, 2, 512], F32, 1)
    mt = Ring(P, "mt", [128, 2, 512], BF16, 2)
    md_o = [P.objs(9, f"md{i}_") for i in range(4)]
    hvv = hT_d.rearrange("(k p) t -> p k t", p=128)

    TILES = [(t0, min(512, TA - t0)) for t0 in range(0, TA, 512)]

    for slot in slots:
        if stage == 4 and slot == 1:
            raise StopBuild(P, nc)
        is_ret = slot < 2
        P.dma("sp", lambda e, slot=slot: [e.dma_start(out=wsl[:, q * 4:(q + 1) * 4, :], in_=win[slot].rearrange("p (k n) -> p k n", k=KC)[:, q * 4:(q + 1) * 4, :]) for q in range(4)],
              writes=[wsl_o], n=4)
        for (t0, n) in TILES:
            for s0 in range(0, n, 256):
                pass
            hts = []
            for s0 in range(0, n, 256):
                ht, ht_o = hst.next()
                P.dma("sp", lambda e, ht=ht, t0=t0, s0=s0: e.dma_start(out=ht[:, :, :], in_=hvv[:, :, t0 + s0:t0 + s0 + 256]), reads=[hTd_o[(t0 + s0) // 256]], writes=[ht_o])
                hts.append((ht, ht_o, s0))
            for which, dst, dst_o in ((0, qT, qT_o), (1, kT, kT_o)):
                for (ht, ht_o, s0) in hts:
                    ps, ps_o = pp.next()
                    for kc in range(KC):
                        mm(P, ps[:, :256], ps_o, wsl[:, kc, which * 128:(which + 1) * 128], wsl_o, ht[:, kc, :], ht_o, kc == 0, kc == KC - 1)
                    tt = t0 + s0
                    scale = (128 ** -0.5) if which == 0 else 1.0
                    if is_ret and tt >= 256:
                        p0 = tt - 256
                        q_, q_o = qf.next()
                        V(P, "act", "activation", [ps_o], [q_o], out=q_[:, :256], in_=ps[:, :256], func=AF.Copy, scale=scale)
                        ps2, ps2_o = pp.next()
                        mm(P, ps2[:, :256], ps2_o, PERM, cs_o, q_[:, :256], q_o, True, True)
                        rcb, rcb_o = rc_.next()
                        rsb, rsb_o = rs_.next()
                        P.dma("sp", lambda e, rcb=rcb, p0=p0: e.dma_start(out=rcb[:, :256], in_=ropec[:, p0:p0 + 256]), writes=[rcb_o])
                        P.dma("sp", lambda e, rsb=rsb, p0=p0: e.dma_start(out=rsb[:, :256], in_=ropes[:, p0:p0 + 256]), writes=[rsb_o])
                        a1, a1_o = t1.next()
                        a2, a2_o = t2.next()
                        V(P, "pool", "tensor_tensor", [q_o, rcb_o], [a1_o], out=a1[:, :256], in0=q_[:, :256], in1=rcb[:, :256], op=ALU.mult)
                        V(P, "dve", "tensor_tensor", [ps2_o, rsb_o], [a2_o], out=a2[:, :256], in0=ps2[:, :256], in1=rsb[:, :256], op=ALU.mult)
                        V(P, "dve", "tensor_tensor", [a1_o, a2_o], [dst_o[tt // 256]], out=dst[:, tt:tt + 256], in0=a1[:, :256], in1=a2[:, :256], op=ALU.add)
                    else:
                        V(P, "act", "activation", [ps_o], [dst_o[tt // 256]], out=dst[:, tt:tt + 256], in_=ps[:, :256], func=AF.Copy, scale=scale)
            for (ht, ht_o, s0) in hts:
                for j in range(2):
                    ch = (t0 + s0) // 128 + j
                    ps, ps_o = pp.next()
                    for kc in range(KC):
                        mm(P, ps[:, :264], ps_o, ht[:, kc, j * 128:(j + 1) * 128], ht_o, wsl[:, kc, 512:776], wsl_o, kc == 0, kc == KC - 1)
                    V(P, "act", "copy", [ps_o], [vtm_o], out=vtm[:, ch, :], in_=ps[:, 0:256])
                    if not is_ret:
                        V(P, "act", "copy", [ps_o], [gt_o], out=gt[:, ch, :], in_=ps[:, 256:260])
        for ch in range(NCH):
            ps, ps_o = pp.next()
            mm(P, ps[:, :128], ps_o, kT[:, ch * 128:(ch + 1) * 128], kT_o[ch // 2], identb[:, :], identb_o, True, True)
            V(P, "act", "copy", [ps_o], [ktm_o], out=ktm[:, ch, :], in_=ps[:, :128])
        if stage == 1:
            raise StopBuild(P, nc)
        if is_ret:
            V(P, "act", "activation", [gp_o], [gts_o], out=lg2[:, :], in_=gp[:, slot, 0:2], func=AF.Exp, scale=-1.0)
            V(P, "act", "activation", [gts_o, one1_o], [gts_o], out=lg2[:, :], in_=lg2[:, :], func=AF.Ln, bias=one1[:, 0:1])
            for d_ in range(2):
                V(P, "dve", "tensor_scalar", [gts_o, onesf_o], [gts_o], out=logf[:, d_ * NCH:(d_ + 1) * NCH], in0=onesf[:, :NCH], scalar1=lg2[:, d_:d_ + 1], scalar2=-1.0, op0=ALU.mult, op1=ALU.mult)
            V(P, "pool", "memset", [], [gts_o], ipre[:, :], 0.0)
        else:
            for d_ in range(2):
                V(P, "dve", "tensor_scalar", [gt_o, gp_o], [gts_o], out=ipre[:, d_ * NCH:(d_ + 1) * NCH], in0=gt[:, :, 2 * d_], scalar1=gp[:, slot, 2 * d_:2 * d_ + 1], scalar2=None, op0=ALU.add)
                V(P, "dve", "tensor_scalar", [gt_o, gp_o], [gts_o], out=logf[:, d_ * NCH:(d_ + 1) * NCH], in0=gt[:, :, 2 * d_ + 1], scalar1=gp[:, slot, 2 * d_ + 1:2 * d_ + 2], scalar2=None, op0=ALU.add)
            V(P, "act", "activation", [gts_o], [gts_o], out=logf[:, :], in_=logf[:, :], func=AF.Exp, scale=-1.0)
            V(P, "act", "activation", [gts_o, one1_o], [gts_o], out=logf[:, :], in_=logf[:, :], func=AF.Ln, bias=one1[:, 0:1])
            V(P, "dve", "tensor_scalar", [gts_o], [gts_o], out=logf[:, :], in0=logf[:, :], scalar1=-1.0, scalar2=None, op0=ALU.mult)
        psb, psb_o = pp.next()
        for d_ in range(2):
            mm(P, psb[:, d_ * NCH:(d_ + 1) * NCH], psb_o, TRI[d_], cs_o, logf[:, d_ * NCH:(d_ + 1) * NCH], gts_o, True, True)
        mm(P, psb[:, 128:128 + 2 * NCH], psb_o, onesf[:, :], onesf_o, logf[:, :], gts_o, True, True)
        V(P, "dve", "tensor_copy", [psb_o], [gts_o], out=btm[:, :], in_=psb[:, 0:2 * NCH])
        V(P, "dve", "tensor_tensor", [gts_o], [gts_o], out=ekey[:, :], in0=ipre[:, :], in1=btm[:, :], op=ALU.subtract)
        V(P, "dve", "tensor_tensor", [gts_o, psb_o], [gts_o], out=wst[:, :], in0=psb[:, 128:128 + 2 * NCH], in1=ekey[:, :], op=ALU.add)
        V(P, "act", "activation", [gts_o], [gts_o], out=wst[:, :], in_=wst[:, :], func=AF.Exp)
        V(P, "act", "activation", [gts_o, psb_o], [gts_o], out=EB[:, :], in_=psb[:, 128:128 + 2 * NCH], func=AF.Exp)

        if stage == 2:
            raise StopBuild(P, nc)
        for d_ in range(2):
            order = [0, 1] + list(range(2, NCH)) if d_ == 0 else [1, 0] + list(range(NCH - 1, 1, -1))
            for ci, ch in enumerate(order):
                first = ci == 0
                tok = slice(ch * 128, (ch + 1) * 128)
                lr, lr_o = lrow.next()
                V(P, "pool", "tensor_scalar", [gts_o, cs_o], [lr_o], out=lr[:, :], in0=TRI[d_], scalar1=logf[:, d_ * NCH + ch:d_ * NCH + ch + 1], scalar2=None, op0=ALU.mult)
                pa, pa_o = pp.next()
                mm(P, pa[:, 0:128], pa_o, onesf[:, :], onesf_o, lr[:, :], lr_o, True, True)
                mm(P, pa[:, 128:256], pa_o, kT[:, tok], kT_o[ch // 2], qT[:, tok], qT_o[ch // 2], True, True)
                bm, bm_o = brm.next()
                V(P, "dve", "tensor_tensor", [pa_o, cs_o], [bm_o], out=bm[:, :], in0=pa[:, 0:128], in1=NMASK[d_], op=ALU.add)
                w_, w_o = Wt.next()
                V(P, "act", "activation", [bm_o, gts_o], [w_o], out=w_[:, :], in_=bm[:, :], func=AF.Exp, bias=ekey[:, d_ * NCH + ch:d_ * NCH + ch + 1])
                pt, pt_o = PT.next()
                V(P, "dve", "tensor_tensor", [pa_o, w_o], [pt_o], out=pt[:, :], in0=pa[:, 128:256], in1=w_[:, :], op=ALU.mult)
                if not first:
                    eb, eb_o = Eb.next()
                    V(P, "act", "activation", [pa_o], [eb_o], out=eb[:, :], in_=pa[:, 0:128], func=AF.Exp)
                    q_s, q_so = qs.next()
                    V(P, "pool", "tensor_tensor", [qT_o[ch // 2], eb_o], [q_so], out=q_s[:, :], in0=qT[:, tok], in1=eb[:, :], op=ALU.mult)
                pb, pb_o = pp.next()
                for half in range(2):
                    mm(P, pb[:, half * 128:(half + 1) * 128], pb_o, vtm[:, ch, half * 128:(half + 1) * 128], vtm_o, pt[:, :], pt_o, True, first)
                    if not first:
                        mm(P, pb[:, half * 128:(half + 1) * 128], pb_o, Stb[:, half * 128:(half + 1) * 128], Stb_o, q_s[:, :], q_so, False, True)
                if not is_ret:
                    mm(P, pb[:, 256:384], pb_o, onesb[:, :], onesb_o, pt[:, :], pt_o, True, first)
                    if not first:
                        mm(P, pb[:, 256:384], pb_o, Stb[:, 256:384], Stb_o, q_s[:, :], q_so, False, True)
                    d1, d1_o = dd.next()
                    V(P, "act", "activation", [pb_o], [d1_o], out=d1[:, :], in_=pb[:, 256:384], func=AF.Abs)
                    V(P, "dve", "tensor_scalar", [d1_o], [d1_o], out=d1[:, :], in0=d1[:, :], scalar1=1.0, scalar2=None, op0=ALU.max)
                    V(P, "dve", "reciprocal", [d1_o], [d1_o], out=d1[:, :], in_=d1[:, :])
                for half in range(2):
                    src = pb[:, half * 128:(half + 1) * 128]
                    if is_ret:
                        if d_ == 0:
                            V(P, "act", "copy", [pb_o], [hacc_o[ch]], out=hacc[:, half, tok], in_=src)
                        else:
                            V(P, "dve", "tensor_tensor", [pb_o, hacc_o[ch]], [hacc_o[ch]], out=hacc[:, half, tok], in0=src, in1=hacc[:, half, tok], op=ALU.add)
                    else:
                        if d_ == 0:
                            V(P, "dve", "tensor_tensor", [pb_o, d1_o], [hacc_o[ch]], out=hacc[:, half, tok], in0=src, in1=d1[:, :], op=ALU.mult)
                        else:
                            h1, h1_o = hn.next()
                            V(P, "dve", "tensor_tensor", [pb_o, d1_o], [h1_o], out=h1[:, :], in0=src, in1=d1[:, :], op=ALU.mult)
                            V(P, "pool", "tensor_tensor", [h1_o, hacc_o[ch]], [hacc_o[ch]], out=hacc[:, half, tok], in0=h1[:, :], in1=hacc[:, half, tok], op=ALU.add)
                if ci < len(order) - 1:
                    k_w, k_wo = kw.next()
                    V(P, "pool", "tensor_scalar", [ktm_o, gts_o], [k_wo], out=k_w[:, :], in0=ktm[:, ch, :], scalar1=wst[:, d_ * NCH + ch:d_ * NCH + ch + 1], scalar2=None, op0=ALU.mult)
                    pc, pc_o = pp.next()
                    mm(P, pc[:, 0:256], pc_o, k_w[:, :], k_wo, vtm[:, ch, :], vtm_o, True, True)
                    if not is_ret:
                        mm(P, pc[:, 256:384], pc_o, k_w[:, :], k_wo, onesb[:, :], onesb_o, True, True)
                    W_ = 256 if is_ret else 384
                    if first:
                        V(P, "dve", "tensor_copy", [pc_o], [St_o], out=St[:, :W_], in_=pc[:, :W_])
                    else:
                        V(P, "dve", "scalar_tensor_tensor", [pc_o, St_o, gts_o], [St_o], out=St[:, :W_], in0=St[:, :W_], scalar=EB[:, d_ * NCH + ch:d_ * NCH + ch + 1], in1=pc[:, :W_], op0=ALU.mult, op1=ALU.add)
                    V(P, "act", "copy", [St_o], [Stb_o], out=Stb[:, :W_], in_=St[:, :W_])

        if stage == 3:
            raise StopBuild(P, nc)
        for (t0, n) in TILES:
            ch0 = t0 // 128
            hobjs = hacc_o[ch0:ch0 + n // 128]
            sq_, sq_o = sqf.next()
            V(P, "act", "activation", hobjs, [sq_o], out=sq_[:, :, :n], in_=hacc[:, :, t0:t0 + n], func=AF.Square)
            ps1, ps1_o = pp.next()
            ps2, ps2_o = pp.next()
            for half in range(2):
                mm(P, ps1[:, :n], ps1_o, onesf[:, :], onesf_o, hacc[:, half, t0:t0 + n], hobjs, half == 0, half == 1)
            for half in range(2):
                mm(P, ps2[:, :n], ps2_o, onesf[:, :], onesf_o, sq_[:, half, :n], sq_o, half == 0, half == 1)
            mn, mn_o = mean.next()
            rsd, rsd_o = rstd.next()
            V(P, "act", "activation", [ps1_o], [mn_o], out=mn[:, :n], in_=ps1[:, :n], func=AF.Copy, scale=1.0 / 256)
            V(P, "dve", "tensor_tensor", [mn_o], [rsd_o], out=rsd[:, :n], in0=mn[:, :n], in1=mn[:, :n], op=ALU.mult)
            V(P, "dve", "scalar_tensor_tensor", [ps2_o, rsd_o], [rsd_o], out=rsd[:, :n], in0=ps2[:, :n], scalar=1.0 / 256, in1=rsd[:, :n], op0=ALU.mult, op1=ALU.subtract)
            V(P, "act", "activation", [rsd_o, c["eps_o"]], [rsd_o], out=rsd[:, :n], in_=rsd[:, :n], func=AF.Sqrt, bias=c["eps"][:, 0:1])
            V(P, "dve", "reciprocal", [rsd_o], [rsd_o], out=rsd[:, :n], in_=rsd[:, :n])
            og, og_o = ogb.next()
            for s0 in range(0, n, 256):
                ht, ht_o = hst.next()
                P.dma("sp", lambda e, ht=ht, t0=t0, s0=s0: e.dma_start(out=ht[:, :, :], in_=hvv[:, :, t0 + s0:t0 + s0 + 256]), reads=[hTd_o[(t0 + s0) // 256]], writes=[ht_o])
                for half in range(2):
                    ps, ps_o = pp.next()
                    for kc in range(KC):
                        mm(P, ps[:, :256], ps_o, wsl[:, kc, 256 + half * 128:256 + (half + 1) * 128], wsl_o, ht[:, kc, :], ht_o, kc == 0, kc == KC - 1)
                    V(P, "act", "activation", [ps_o], [og_o], out=og[:, half, s0:s0 + 256], in_=ps[:, :256], func=(AF.Silu if is_ret else AF.Sigmoid))
            m_, m_o = mt.next()
            for half in range(2):
                V(P, "dve", "tensor_tensor", hobjs + [mn_o], [sq_o], out=sq_[:, half, :n], in0=hacc[:, half, t0:t0 + n], in1=mn[:, :n], op=ALU.subtract)
                V(P, "dve", "tensor_tensor", [sq_o, rsd_o], [sq_o], out=sq_[:, half, :n], in0=sq_[:, half, :n], in1=rsd[:, :n], op=ALU.mult)
                V(P, "dve", "scalar_tensor_tensor", [sq_o, og_o, hg_o], [m_o], out=m_[:, half, :n], in0=sq_[:, half, :n], scalar=hg[:, slot * 2 + half:slot * 2 + half + 1], in1=og[:, half, :n], op0=ALU.mult, op1=ALU.mult)
            P.dma("pool", lambda e, m_=m_, t0=t0, n=n, slot=slot: [e.dma_start(out=m_d[slot * 2 + half, :, t0:t0 + n], in_=m_[:, half, :n]) for half in range(2)],
                  reads=[m_o], writes=[md_o[slot][t0 // 512]], holder=m_o, n=2)

    if stage == 5:
        raise StopBuild(P, nc)
    mo = Ring(P, "mo", [128, 8, 512], BF16, 1)
    wo = Ring(P, "wo", [128, 8, 128], BF16, 2)
    yo = Ring(P, "yo", [128, 512], F32, 2)
    for (t0, n) in TILES:
        m8, m8_o = mo.next()
        P.dma("sp", lambda e, m8=m8, t0=t0, n=n: [e.dma_start(out=m8[:, r, :n], in_=m_d[r, :, t0:t0 + n]) for r in range(8)], reads=[md_o[s_][t0 // 512] for s_ in range(4)], writes=[m8_o], n=8)
        for dc in range(KC):
            w8, w8_o = wo.next()
            P.dma("sp", lambda e, w8=w8, dc=dc: e.dma_start(out=w8[:, :, :], in_=wout[dc].rearrange("p (r n) -> p r n", r=8)), writes=[w8_o])
            ps, ps_o = pp.next()
            for r in range(8):
                mm(P, ps[:, :n], ps_o, w8[:, r, :], w8_o, m8[:, r, :n], m8_o, r == 0, r == 7)
            y_, y_o = yo.next()
            V(P, "act", "copy", [ps_o], [y_o], out=y_[:, :n], in_=ps[:, :n])
            P.dma("pool", lambda e, y_=y_, dc=dc, t0=t0, n=n: e.dma_start(out=yT[dc * 128:(dc + 1) * 128, t0:t0 + n], in_=y_[:, :n]), reads=[y_o])
    P.emit()
    return nc


NL = 4096
XBW = 4360
LT = [(0, 256)] + [(256 + 512 * k, 512) for k in range(8)]


def build_cda():
    nc = bass.Bass("TRN2", target_bir_lowering=False)
    di = lambda n, s, d: nc.dram_tensor(n, s, d, kind="ExternalInput").ap()
    xT = di("xT", [D, TA], F32)
    modv = di("modv", [128, 96], F32)
    gn = di("gn", [128, KC], F32)
    wlru = di("wlru", [4, 128, KC * 256], BF16)
    wgate = di("wgate", [4, 128, 4 * 128], BF16)
    lpar = di("lpar", [128, 4 * 12], F32)
    mout = nc.dram_tensor("mlru", [4, 128, NL], BF16, kind="ExternalOutput").ap()
    hT_d = nc.dram_tensor("hT", [D, TA], BF16, kind="ExternalOutput").ap()
    P = Prog(nc)
    pp = PsumPool(P)
    c = ffn_consts(P, modv, gn)
    add_eps(P, c)
    nb = norm_bufs(P)
    one1 = P.sbuf("one1", [128, 1], F32); one1_o = P.obj("one1")
    V(P, "pool", "memset", [], [one1_o], one1[:, :], 1.0)
    lp = P.sbuf("lpar_sb", [128, 4, 12], F32); lp_o = P.obj("lpar")
    P.dma("sp", lambda e: e.dma_start(out=lp[:, :, :], in_=lpar.rearrange("p (a b) -> p a b", a=4)), writes=[lp_o])
    hst = Ring(P, "hst", [128, KC, 256], BF16, 2)
    hTd_o = P.objs(TA // 256, "hTd")
    hv = hT_d.rearrange("(k p) t -> p k t", p=128)
    for t0 in range(0, TA, 256):
        g = 1 if t0 < 256 else 0
        ht, ht_o = hst.next()
        norm_mod(P, pp, c, xT, t0, 256, g, ht, ht_o, 0, nb)
        P.dma("pool", lambda e, ht=ht, t0=t0: e.dma_start(out=hv[:, :, t0:t0 + 256], in_=ht[:, :, :]), reads=[ht_o], writes=[hTd_o[t0 // 256]], holder=ht_o)
    wl = P.sbuf("wl", [128, KC, 256], BF16); wl_o = P.obj("wl")
    wg = P.sbuf("wg", [128, 4, 128], BF16); wg_o = P.obj("wg")
    XB = P.sbuf("XB", [128, XBW], F32); XB_o = P.obj("XB")
    XL = P.sbuf("XL", [128, TA], F32); XL_o = P.obj("XL")
    XLb = P.sbuf("XLb", [128, TA], BF16); XLb_o = P.obj("XLb")
    gy = P.sbuf("gy", [128, NL], BF16); gy_o = P.obj("gy")
    HF = P.sbuf("HF", [128, TA], F32); HF_o = P.obj("HF")
    HB = P.sbuf("HB", [128, TA], F32); HB_o = P.obj("HB")
    cl = P.sbuf("cl", [128, 4], F32); cl_o = P.obj("cl")
    r_ = Ring(P, "r", [128, 512], F32, 2)
    i_ = Ring(P, "i", [128, 512], F32, 2)
    a_ = Ring(P, "a", [128, 512], F32, 2)
    a2_ = Ring(P, "a2", [128, 512], F32, 2)
    b_ = Ring(P, "b", [128, 512], F32, 2)
    g1 = Ring(P, "g1", [128, 256], F32, 2)
    g2 = Ring(P, "g2", [128, 256], F32, 2)
    mo = Ring(P, "mo", [128, 512], BF16, 2)
    V(P, "pool", "memset", [], [XB_o], XB[:, :], 0.0)
    for blk in range(4):
        P.dma("sp", lambda e, blk=blk: e.dma_start(out=wl[:, :, :], in_=wlru[blk].rearrange("p (k n) -> p k n", k=KC)), writes=[wl_o])
        P.dma("sp", lambda e, blk=blk: e.dma_start(out=wg[:, :, :], in_=wgate[blk].rearrange("p (a n) -> p a n", a=4)), writes=[wg_o])
        for d_ in range(2):
            lam = lp[:, blk, 7 + 3 * d_:8 + 3 * d_]
            V(P, "act", "activation", [lp_o], [cl_o], out=cl[:, 2 * d_:2 * d_ + 1], in_=lam, func=AF.Exp, scale=-1.0)
            V(P, "act", "activation", [cl_o, one1_o], [cl_o], out=cl[:, 2 * d_:2 * d_ + 1], in_=cl[:, 2 * d_:2 * d_ + 1], func=AF.Ln, bias=one1[:, 0:1])
            V(P, "dve", "tensor_scalar", [cl_o], [cl_o], out=cl[:, 2 * d_ + 1:2 * d_ + 2], in0=cl[:, 2 * d_:2 * d_ + 1], scalar1=-16.0, scalar2=None, op0=ALU.mult)
            V(P, "dve", "tensor_scalar", [cl_o], [cl_o], out=cl[:, 2 * d_:2 * d_ + 1], in0=cl[:, 2 * d_:2 * d_ + 1], scalar1=-8.0, scalar2=None, op0=ALU.mult)
        for t0 in range(0, TA, 256):
            ht, ht_o = hst.next()
            P.dma("sp", lambda e, ht=ht, t0=t0: e.dma_start(out=ht[:, :, :], in_=hv[:, :, t0:t0 + 256]), reads=[hTd_o[t0 // 256]], writes=[ht_o])
            ps, ps_o = pp.next()
            for kc in range(KC):
                mm(P, ps[:, :256], ps_o, wl[:, kc, 128:256], wl_o, ht[:, kc, :], ht_o, kc == 0, kc == KC - 1)
            col = 2 + t0 if t0 < 256 else 261 + (t0 - 256)
            V(P, "act", "copy", [ps_o], [XB_o], out=XB[:, col:col + 256], in_=ps[:, :256])
            if t0 >= 256:
                p0 = t0 - 256
                ps2, ps2_o = pp.next()
                for kc in range(KC):
                    mm(P, ps2[:, :256], ps2_o, wl[:, kc, 0:128], wl_o, ht[:, kc, :], ht_o, kc == 0, kc == KC - 1)
                u, u_o = g1.next()
                w, w_o = g2.next()
                V(P, "act", "activation", [ps2_o], [u_o], out=u[:, :], in_=ps2[:, :256], func=AF.Square)
                V(P, "dve", "tensor_scalar", [u_o], [u_o], out=u[:, :], in0=u[:, :], scalar1=0.044715, scalar2=1.0, op0=ALU.mult, op1=ALU.add)
                V(P, "dve", "tensor_tensor", [u_o, ps2_o], [u_o], out=u[:, :], in0=ps2[:, :256], in1=u[:, :], op=ALU.mult)
                V(P, "act", "activation", [u_o], [w_o], out=w[:, :], in_=u[:, :], func=AF.Sigmoid, scale=1.5957691216)
                V(P, "dve", "tensor_tensor", [w_o, ps2_o], [gy_o], out=gy[:, p0:p0 + 256], in0=ps2[:, :256], in1=w[:, :], op=ALU.mult)
        for (oin, oout, N) in ((0, 0, 256), (259, 256, NL)):
            V(P, "dve", "tensor_scalar", [XB_o, lp_o], [XL_o], out=XL[:, oout:oout + N], in0=XB[:, oin + 2:oin + 2 + N],
              scalar1=lp[:, blk, 2:3], scalar2=lp[:, blk, 4:5], op0=ALU.mult, op1=ALU.add)
            for j in (0, 1, 3):
                V(P, "dve", "scalar_tensor_tensor", [XB_o, lp_o, XL_o], [XL_o], out=XL[:, oout:oout + N], in0=XB[:, oin + j:oin + j + N],
                  scalar=lp[:, blk, j:j + 1], in1=XL[:, oout:oout + N], op0=ALU.mult, op1=ALU.add)
        V(P, "pool", "tensor_copy", [XL_o], [XLb_o], out=XLb[:, :], in_=XL[:, :])
        for d_ in range(2):
            H, H_o = (HF, HF_o) if d_ == 0 else (HB, HB_o)
            order = LT if d_ == 0 else [LT[0]] + LT[:0:-1]
            prev = None
            for (t0, n) in order:
                ps, ps_o = pp.next()
                mm(P, ps[:, :n], ps_o, wg[:, 2 * d_, :], wg_o, XLb[:, t0:t0 + n], XLb_o, True, True)
                ps2, ps2_o = pp.next()
                mm(P, ps2[:, :n], ps2_o, wg[:, 2 * d_ + 1, :], wg_o, XLb[:, t0:t0 + n], XLb_o, True, True)
                r, r_o = r_.next()
                ii, ii_o = i_.next()
                a, a_o = a_.next()
                a2, a2_o = a2_.next()
                b, b_o = b_.next()
                V(P, "act", "activation", [ps_o, lp_o], [r_o], out=r[:, :n], in_=ps[:, :n], func=AF.Sigmoid, bias=lp[:, blk, 5 + 3 * d_:6 + 3 * d_])
                V(P, "act", "activation", [ps2_o, lp_o], [ii_o], out=ii[:, :n], in_=ps2[:, :n], func=AF.Sigmoid, bias=lp[:, blk, 6 + 3 * d_:7 + 3 * d_])
                V(P, "act", "activation", [r_o, cl_o], [a_o], out=a[:, :n], in_=r[:, :n], func=AF.Exp, scale=cl[:, 2 * d_:2 * d_ + 1])
                V(P, "act", "activation", [r_o, cl_o], [a2_o], out=a2[:, :n], in_=r[:, :n], func=AF.Exp, scale=cl[:, 2 * d_ + 1:2 * d_ + 2])
                V(P, "dve", "tensor_scalar", [a2_o], [a2_o], out=a2[:, :n], in0=a2[:, :n], scalar1=-1.0, scalar2=1.0, op0=ALU.mult, op1=ALU.add)
                V(P, "act", "activation", [a2_o], [a2_o], out=a2[:, :n], in_=a2[:, :n], func=AF.Sqrt)
                V(P, "dve", "tensor_tensor", [a2_o, ii_o], [b_o], out=b[:, :n], in0=a2[:, :n], in1=ii[:, :n], op=ALU.mult)
                V(P, "dve", "tensor_tensor", [b_o, XL_o], [b_o], out=b[:, :n], in0=b[:, :n], in1=XL[:, t0:t0 + n], op=ALU.mult)
                if d_ == 0:
                    init = 0.0 if prev is None else H[:, prev:prev + 1]
                    V(P, "dve", "tensor_tensor_scan", [a_o, b_o, H_o], [H_o], out=H[:, t0:t0 + n], data0=a[:, :n], data1=b[:, :n], initial=init, op0=ALU.mult, op1=ALU.add)
                    prev = t0 + n - 1
                else:
                    init = 0.0 if prev is None else H[:, prev:prev + 1]
                    V(P, "dve", "tensor_tensor_scan", [a_o, b_o, H_o], [H_o], out=H[:, t0 + n - 1:(t0 - 1 if t0 > 0 else None):-1],
                      data0=a[:, n - 1::-1] if n == 512 else a[:, n - 1::-1], data1=b[:, n - 1::-1], initial=init, op0=ALU.mult, op1=ALU.add)
                    prev = t0
        for p0 in range(0, NL, 512):
            m, m_o = mo.next()
            u, u_o = r_.next()
            V(P, "pool", "tensor_tensor", [HF_o, HB_o], [u_o], out=u[:, :], in0=HF[:, 256 + p0:256 + p0 + 512], in1=HB[:, 256 + p0:256 + p0 + 512], op=ALU.add)
            V(P, "dve", "tensor_tensor", [u_o, gy_o], [m_o], out=m[:, :], in0=u[:, :], in1=gy[:, p0:p0 + 512], op=ALU.mult)
            P.dma("pool", lambda e, m=m, p0=p0, blk=blk: e.dma_start(out=mout[blk, :, p0:p0 + 512], in_=m[:, :]), reads=[m_o])
    P.emit()
    return nc


NL = 4096
TB1 = 2176
B1_TILES = [(0, 128)] + [(128 + 256 * k, 256) for k in range(8)]
QSCALE = 192 ** -0.5
SHIFT = -8.0


def build_cdb1():
    nc = bass.Bass("TRN2", target_bir_lowering=False)
    di = lambda n, s, d: nc.dram_tensor(n, s, d, kind="ExternalInput").ap()
    do = lambda n, s, d: nc.dram_tensor(n, s, d, kind="ExternalOutput").ap()
    hT = di("hT", [D, TB1], BF16)
    wmla = di("wmla", [128, KC * 832], BF16)
    wuv = di("wuv", [128, 2 * 1024], BF16)
    npar = di("npar", [128, 6], F32)
    ckvn_d = do("ckvn", [2, 128, TB1], BF16)
    cqn_d = do("cqn", [4, 128, TB1 - 128], BF16)
    kr_d = do("kr", [64, TB1], BF16)
    v_d = do("v", [17, 128, 1024], BF16)
    P = Prog(nc)
    pp = PsumPool(P)
    c = {}
    c["ones"] = P.sbuf("ones_bf", [128, 128], BF16); c["ones_o"] = P.obj("ones")
    V(P, "pool", "memset", [], [c["ones_o"]], c["ones"][:, :], 1.0)
    add_eps(P, c)
    wm = P.sbuf("wm", [128, KC, 832], BF16); wm_o = P.obj("wm")
    P.dma("sp", lambda e: [e.dma_start(out=wm[:, q * 4:(q + 1) * 4, :], in_=wmla.rearrange("p (k n) -> p k n", k=KC)[:, q * 4:(q + 1) * 4, :]) for q in range(4)], writes=[wm_o], n=4)
    wv = P.sbuf("wv", [128, 2, 1024], BF16); wv_o = P.obj("wv")
    P.dma("sp", lambda e: e.dma_start(out=wv[:, :, :], in_=wuv.rearrange("p (k n) -> p k n", k=2)), writes=[wv_o])
    npr = P.sbuf("npr", [128, 6], F32); npr_o = P.obj("npr")
    P.dma("sp", lambda e: e.dma_start(out=npr[:, :], in_=npar), writes=[npr_o])
    hst = Ring(P, "hst", [128, KC, 256], BF16, 2)
    xf = Ring(P, "xf", [128, 4, 256], F32, 2)
    sq = Ring(P, "sq", [128, 4, 256], BF16, 2)
    rs = Ring(P, "rs", [128, 256], F32, 2)
    on = Ring(P, "on", [128, 4, 256], BF16, 2)
    krt = Ring(P, "krt", [64, 256], BF16, 2)
    vt = Ring(P, "vt", [128, 1024], BF16, 2)
    hv = hT.rearrange("(k p) t -> p k t", p=128)

    def proj_norm(ht, ht_o, n, col0, nk, gcol, dim):
        x_, x_o = xf.next()
        s_, s_o = sq.next()
        for k2 in range(nk):
            ps, ps_o = pp.next()
            for kc in range(KC):
                mm(P, ps[:, :n], ps_o, wm[:, kc, col0 + k2 * 128:col0 + (k2 + 1) * 128], wm_o, ht[:, kc, :n], ht_o, kc == 0, kc == KC - 1)
            V(P, "act", "copy", [ps_o], [x_o], out=x_[:, k2, :n], in_=ps[:, :n])
            V(P, "act", "activation", [ps_o], [s_o], out=s_[:, k2, :n], in_=ps[:, :n], func=AF.Square)
        ps, ps_o = pp.next()
        for k2 in range(nk):
            mm(P, ps[:, :n], ps_o, c["ones"][:, :], c["ones_o"], s_[:, k2, :n], s_o, k2 == 0, k2 == nk - 1)
        r_, r_o = rs.next()
        V(P, "act", "activation", [ps_o, c["eps_o"]], [r_o], out=r_[:, :n], in_=ps[:, :n], func=AF.Sqrt, bias=c["eps"][:, 0:1], scale=1.0 / dim)
        V(P, "dve", "reciprocal", [r_o], [r_o], out=r_[:, :n], in_=r_[:, :n])
        o_, o_o = on.next()
        for k2 in range(nk):
            V(P, "dve", "scalar_tensor_tensor", [x_o, r_o, npr_o], [o_o], out=o_[:, k2, :n], in0=x_[:, k2, :n], scalar=npr[:, gcol + k2:gcol + k2 + 1], in1=r_[:, :n], op0=ALU.mult, op1=ALU.mult)
        return o_, o_o

    for (t0, n) in B1_TILES:
        ht, ht_o = hst.next()
        P.dma("sp", lambda e, ht=ht, t0=t0, n=n: e.dma_start(out=ht[:, :, :n], in_=hv[:, :, t0:t0 + n]), writes=[ht_o])
        o_, o_o = proj_norm(ht, ht_o, n, 512, 2, 4, 256.0)
        P.dma("pool", lambda e, o_=o_, t0=t0, n=n: [e.dma_start(out=ckvn_d[k2, :, t0:t0 + n], in_=o_[:, k2, :n]) for k2 in range(2)], reads=[o_o], n=2)
        for j in range(n // 128):
            v_, v_o = vt.next()
            for half in range(2):
                ps, ps_o = pp.next()
                for k2 in range(2):
                    mm(P, ps[:, :512], ps_o, o_[:, k2, j * 128:(j + 1) * 128], o_o, wv[:, k2, half * 512:(half + 1) * 512], wv_o, k2 == 0, k2 == 1)
                if half == 0:
                    V(P, "act", "copy", [ps_o], [v_o], out=v_[:, 0:512], in_=ps[:, :512])
                else:
                    V(P, "dve", "tensor_copy", [ps_o], [v_o], out=v_[:, 512:1024], in_=ps[:, :512])
            ch = (t0 // 128) + j
            P.dma("pool", lambda e, v_=v_, ch=ch: e.dma_start(out=v_d[ch], in_=v_[:, :]), reads=[v_o])
        ps, ps_o = pp.next()
        for kc in range(KC):
            mm(P, ps[0:64, :n], ps_o, wm[:, kc, 768:832], wm_o, ht[:, kc, :n], ht_o, kc == 0, kc == KC - 1)
        k_, k_o = krt.next()
        V(P, "act", "copy", [ps_o], [k_o], out=k_[:, :n], in_=ps[0:64, :n])
        P.dma("pool", lambda e, k_=k_, t0=t0, n=n: e.dma_start(out=kr_d[:, t0:t0 + n], in_=k_[:, :n]), reads=[k_o])
        if t0 >= 128:
            o_, o_o = proj_norm(ht, ht_o, n, 0, 4, 0, 512.0)
            P.dma("pool", lambda e, o_=o_, t0=t0, n=n: [e.dma_start(out=cqn_d[k2, :, t0 - 128:t0 - 128 + n], in_=o_[:, k2, :n]) for k2 in range(4)], reads=[o_o], n=4)
    P.emit()
    return nc


def build_cdb2():
    nc = bass.Bass("TRN2", target_bir_lowering=False)
    di = lambda n, s, d: nc.dram_tensor(n, s, d, kind="ExternalInput").ap()
    ckvn_d = di("ckvn", [2, 128, TA], BF16)
    cqn_d = di("cqn", [4, 128, NL], BF16)
    kr_d = di("kr", [64, TA], BF16)
    v_d = di("v", [NCH, 128, 512], BF16)
    wuq = di("wuq", [128, 4 * 768], BF16)
    wuk = di("wuk", [128, 2 * 512], BF16)
    mpar = di("mpar", [128, 4], F32)
    mcst = di("mcst", [64, 64], F32)
    mrc = di("mrc", [64, NL], F32)
    mrs = di("mrs", [64, NL], F32)
    mlru = di("mlru", [4, 128, NL], BF16)
    wout = di("wout", [KC, 128, 8 * 128], BF16)
    yT = nc.dram_tensor("yT", [D, NL], F32, kind="ExternalOutput").ap()
    att_d = nc.dram_tensor("att_scr", [4, 128, NL], BF16).ap()
    P = Prog(nc)
    pp = PsumPool(P, n=4)
    c = {}
    c["ones"] = P.sbuf("ones_bf", [128, 128], BF16); c["ones_o"] = P.obj("ones")
    V(P, "pool", "memset", [], [c["ones_o"]], c["ones"][:, :], 1.0)
    add_eps(P, c)
    shf = P.sbuf("shf", [128, 1], F32); shf_o = P.obj("shf")
    V(P, "pool", "memset", [], [shf_o], shf[:, :], SHIFT)
    accb = [P.psum(f"acc{i}", [128, 512], F32) for i in range(4)]
    accb_o = P.objs(4, "acc")
    ckvn = P.sbuf("ckvn", [128, 2, TA], BF16); ckvn_o = P.obj("ckvn")
    P.dma("sp", lambda e: [e.dma_start(out=ckvn[:, k, :], in_=ckvn_d[k]) for k in range(2)], writes=[ckvn_o], n=2)
    cqn = P.sbuf("cqn", [128, 4, NL], BF16); cqn_o = P.obj("cqn")
    P.dma("sp", lambda e: [e.dma_start(out=cqn[:, k, :], in_=cqn_d[k]) for k in range(4)], writes=[cqn_o], n=4)
    kr = P.sbuf("kr", [64, TA], BF16); kr_o = P.obj("kr")
    P.dma("sp", lambda e: e.dma_start(out=kr[:, :], in_=kr_d), writes=[kr_o])
    vtm = P.sbuf("vtm", [128, NCH, 512], BF16); vtm_o = P.obj("vtm")
    P.dma("sp", lambda e: [e.dma_start(out=vtm[:, q * 2:(q + 1) * 2, :], in_=v_d[q * 2:(q + 1) * 2].rearrange("c p n -> p c n")) for q in range(NCH // 2)], writes=[vtm_o], n=NCH // 2)
    wq = P.sbuf("wq", [128, 4, 768], BF16); wq_o = P.obj("wq")
    P.dma("sp", lambda e: e.dma_start(out=wq[:, :, :], in_=wuq.rearrange("p (k n) -> p k n", k=4)), writes=[wq_o])
    wk = P.sbuf("wk", [128, 2, 512], BF16); wk_o = P.obj("wk")
    P.dma("sp", lambda e: e.dma_start(out=wk[:, :, :], in_=wuk.rearrange("p (k n) -> p k n", k=2)), writes=[wk_o])
    mp = P.sbuf("mp", [128, 4], F32); mp_o = P.obj("mp")
    P.dma("sp", lambda e: e.dma_start(out=mp[:, :], in_=mpar), writes=[mp_o])
    V(P, "dve", "tensor_scalar", [mp_o], [mp_o], out=mp[:, 0:2], in0=mp[:, 0:2], scalar1=QSCALE, scalar2=None, op0=ALU.mult)
    mperm = P.sbuf("mperm", [64, 64], F32); mperm_o = P.obj("mperm")
    P.dma("sp", lambda e: e.dma_start(out=mperm[:, :], in_=mcst), writes=[mperm_o])
    KN = P.sbuf("KN", [128, TA], BF16); KN_o = P.objs(17, "KN")
    KR = P.sbuf("KRr", [64, TA], BF16); KR_o = P.objs(17, "KRr")
    QN = P.sbuf("QN", [128, NL], BF16); QN_o = P.objs(16, "QN")
    QR = P.sbuf("QR", [64, NL], BF16); QR_o = P.objs(16, "QR")
    nf = Ring(P, "nf", [128, 256], F32, 2)
    rf = Ring(P, "rf", [64, 256], F32, 2)
    sqn = Ring(P, "sqn", [128, 256], BF16, 2)
    sqr = Ring(P, "sqr", [64, 256], BF16, 2)
    rs = Ring(P, "rs", [128, 256], F32, 2)
    rn = Ring(P, "rn", [64, 256], F32, 2)
    tc_ = Ring(P, "tc", [64, 256], F32, 2)
    ts_ = Ring(P, "ts", [64, 256], F32, 2)
    t1 = Ring(P, "t1", [64, 256], F32, 2)
    t2 = Ring(P, "t2", [64, 256], F32, 2)
    pt = Ring(P, "pt", [128, 512], BF16, 3)
    rd = Ring(P, "rd", [128, 512], F32, 2)
    ao = Ring(P, "ao", [128, 512], BF16, 2)
    att_o = [P.objs(8, f"att{h}_") for h in range(4)]

    def prep(nps, nps_o, rsrc, rsrc_o, rsrc_is_psum, gcol, dstN, dstN_o, dstR, dstR_o, tok, pos0):
        n_, n_o = nf.next()
        s_, s_o = sqn.next()
        V(P, "act", "copy", [nps_o], [n_o], out=n_[:, :], in_=nps[:, :256])
        V(P, "act", "activation", [nps_o], [s_o], out=s_[:, :], in_=nps[:, :256], func=AF.Square)
        r_, r_o = rf.next()
        q_, q_o = sqr.next()
        V(P, "act", "copy", [rsrc_o], [r_o], out=r_[:, :], in_=rsrc)
        V(P, "act", "activation", [rsrc_o], [q_o], out=q_[:, :], in_=rsrc, func=AF.Square)
        ps, ps_o = pp.next()
        mm(P, ps[:, :256], ps_o, c["ones"][:, :], c["ones_o"], s_[:, :], s_o, True, False)
        mm(P, ps[:, :256], ps_o, c["ones"][0:64, :], c["ones_o"], q_[:, :], q_o, False, True)
        x_, x_o = rs.next()
        V(P, "act", "activation", [ps_o, c["eps_o"]], [x_o], out=x_[:, :], in_=ps[:, :256], func=AF.Sqrt, bias=c["eps"][:, 0:1], scale=1.0 / 192)
        V(P, "dve", "reciprocal", [x_o], [x_o], out=x_[:, :], in_=x_[:, :])
        V(P, "dve", "scalar_tensor_tensor", [n_o, x_o, mp_o], [dstN_o], out=dstN[:, tok:tok + 256], in0=n_[:, :], scalar=mp[:, gcol:gcol + 1], in1=x_[:, :], op0=ALU.mult, op1=ALU.mult)
        if pos0 is None:
            V(P, "dve", "scalar_tensor_tensor", [r_o, x_o, mp_o], [dstR_o], out=dstR[:, tok:tok + 256], in0=r_[:, :], scalar=mp[0:64, gcol + 1:gcol + 2], in1=x_[0:64, :], op0=ALU.mult, op1=ALU.mult)
            return
        y_, y_o = rn.next()
        V(P, "dve", "scalar_tensor_tensor", [r_o, x_o, mp_o], [y_o], out=y_[:, :], in0=r_[:, :], scalar=mp[0:64, gcol + 1:gcol + 2], in1=x_[0:64, :], op0=ALU.mult, op1=ALU.mult)
        ps2, ps2_o = pp.next()
        mm(P, ps2[0:64, :256], ps2_o, mperm[:, :], mperm_o, y_[:, :], y_o, True, True)
        cb, cb_o = tc_.next()
        sb, sb_o = ts_.next()
        P.dma("sp", lambda e: e.dma_start(out=cb[:, :], in_=mrc[:, pos0:pos0 + 256]), writes=[cb_o])
        P.dma("sp", lambda e: e.dma_start(out=sb[:, :], in_=mrs[:, pos0:pos0 + 256]), writes=[sb_o])
        a1, a1_o = t1.next()
        a2, a2_o = t2.next()
        V(P, "pool", "tensor_tensor", [y_o, cb_o], [a1_o], out=a1[:, :], in0=y_[:, :], in1=cb[:, :], op=ALU.mult)
        V(P, "dve", "tensor_tensor", [ps2_o, sb_o], [a2_o], out=a2[:, :], in0=ps2[0:64, :256], in1=sb[:, :], op=ALU.mult)
        V(P, "dve", "tensor_tensor", [a1_o, a2_o], [dstR_o], out=dstR[:, tok:tok + 256], in0=a1[:, :], in1=a2[:, :], op=ALU.add)

    for h in range(4):
        for ti in range(17):
            tok = ti * 256
            ps, ps_o = pp.next()
            for k2 in range(2):
                mm(P, ps[:, :256], ps_o, wk[:, k2, h * 128:(h + 1) * 128], wk_o, ckvn[:, k2, tok:tok + 256], ckvn_o, k2 == 0, k2 == 1)
            prep(ps, ps_o, kr[:, tok:tok + 256], kr_o, False, 2, KN, KN_o[ti], KR, KR_o[ti], tok, None if ti == 0 else tok - 256)
        for ti in range(16):
            tok = ti * 256
            ps, ps_o = pp.next()
            for k4 in range(4):
                mm(P, ps[:, :256], ps_o, wq[:, k4, h * 192:h * 192 + 128], wq_o, cqn[:, k4, tok:tok + 256], cqn_o, k4 == 0, k4 == 3)
            psr, psr_o = pp.next()
            for k4 in range(4):
                mm(P, psr[0:64, :256], psr_o, wq[:, k4, h * 192 + 128:h * 192 + 192], wq_o, cqn[:, k4, tok:tok + 256], cqn_o, k4 == 0, k4 == 3)
            prep(ps, ps_o, psr[0:64, :256], psr_o, True, 0, QN, QN_o[ti], QR, QR_o[ti], tok, tok)
        for qt in range(8):
            q0 = qt * 512
            num, num_o = accb[2 * (qt % 2)], accb_o[2 * (qt % 2)]
            den, den_o = accb[2 * (qt % 2) + 1], accb_o[2 * (qt % 2) + 1]
            for kc in range(NCH):
                ks = slice(kc * 128, (kc + 1) * 128)
                ps, ps_o = pp.next()
                mm(P, ps[:, :512], ps_o, KN[:, ks], KN_o[kc // 2], QN[:, q0:q0 + 512], QN_o[2 * qt:2 * qt + 2], True, False)
                mm(P, ps[:, :512], ps_o, KR[:, ks], KR_o[kc // 2], QR[:, q0:q0 + 512], QR_o[2 * qt:2 * qt + 2], False, True)
                p_, p_o = pt.next()
                V(P, "act", "activation", [ps_o, shf_o], [p_o], out=p_[:, :], in_=ps[:, :512], func=AF.Exp, bias=shf[:, 0:1])
                mm(P, num[:, :], num_o, vtm[:, kc, h * 128:(h + 1) * 128], vtm_o, p_[:, :], p_o, kc == 0, kc == NCH - 1)
                mm(P, den[:, :], den_o, c["ones"][:, :], c["ones_o"], p_[:, :], p_o, kc == 0, kc == NCH - 1)
            r_, r_o = rd.next()
            V(P, "dve", "reciprocal", [den_o], [r_o], out=r_[:, :], in_=den[:, :])
            a_, a_o = ao.next()
            V(P, "dve", "tensor_tensor", [num_o, r_o], [a_o], out=a_[:, :], in0=num[:, :], in1=r_[:, :], op=ALU.mult)
            P.dma("pool", lambda e, a_=a_, h=h, q0=q0: e.dma_start(out=att_d[h, :, q0:q0 + 512], in_=a_[:, :]), reads=[a_o], writes=[att_o[h][qt]], holder=a_o)
    mo = Ring(P, "mo", [128, 8, 512], BF16, 1)
    wo = Ring(P, "wo", [128, 8, 128], BF16, 2)
    yo = Ring(P, "yo", [128, 512], F32, 2)
    for qt in range(8):
        q0 = qt * 512
        m8, m8_o = mo.next()
        P.dma("sp", lambda e, m8=m8, q0=q0: [e.dma_start(out=m8[:, r, :], in_=mlru[r, :, q0:q0 + 512]) for r in range(4)] +
              [e.dma_start(out=m8[:, 4 + r, :], in_=att_d[r, :, q0:q0 + 512]) for r in range(4)],
              reads=[att_o[h][qt] for h in range(4)], writes=[m8_o], n=8)
        for dc in range(KC):
            w8, w8_o = wo.next()
            P.dma("sp", lambda e, w8=w8, dc=dc: e.dma_start(out=w8[:, :, :], in_=wout[dc].rearrange("p (r n) -> p r n", r=8)), writes=[w8_o])
            ps, ps_o = pp.next()
            for r in range(8):
                mm(P, ps[:, :512], ps_o, w8[:, r, :], w8_o, m8[:, r, :], m8_o, r == 0, r == 7)
            y_, y_o = yo.next()
            V(P, "act", "copy", [ps_o], [y_o], out=y_[:, :], in_=ps[:, :512])
            P.dma("pool", lambda e, y_=y_, dc=dc, q0=q0: e.dma_start(out=yT[dc * 128:(dc + 1) * 128, q0:q0 + 512], in_=y_[:, :]), reads=[y_o])
    P.emit()
    return nc

import numpy as np
import ml_dtypes
bf = ml_dtypes.bfloat16
KC = 16
NEG = -30000.0

def tile_vec(v):
    return np.ascontiguousarray(v.reshape(-1, 128).T)

def ab_consts():
    r = np.arange(128)
    trif = (r[:, None] <= r[None, :]).astype(np.float32)
    trib = (r[:, None] >= r[None, :]).astype(np.float32)
    nmf = np.where(r[:, None] <= r[None, :], 0.0, NEG).astype(np.float32)
    nmb = np.where(r[:, None] >= r[None, :], 0.0, NEG).astype(np.float32)
    perm = (r[:, None] == ((r[None, :] + 64) % 128)).astype(np.float32)
    ident = np.eye(128, dtype=np.float32)
    cst = np.concatenate([trif, trib, nmf, nmb, perm, ident], axis=1)
    freqs = (10000.0 ** (-np.arange(64, dtype=np.float32) / 64)).astype(np.float32)
    ang = np.arange(4096, dtype=np.float32)[:, None] * freqs[None, :]
    cos = np.cos(ang).T.astype(np.float32); sin = np.sin(ang).T.astype(np.float32)
    ropec = np.ascontiguousarray(np.concatenate([cos, cos], axis=0))
    ropes = np.ascontiguousarray(np.concatenate([-sin, sin], axis=0))
    return cst, ropec, ropes

def ab_weights(w_in_b, w_out_b, ret_dl, ret_g, gate_b, ml_g, hg):
    slots = []
    rows = []
    gpar = np.zeros((4, 4), np.float32)
    for s in range(4):
        hd = 2 * hg + (s % 2)
        if s < 2:
            q0, k0, v0, o0 = hd * 128, 512 + hd * 128, 1024 + hd * 256, 2048 + hd * 256
            gates = np.zeros((2048, 8), bf)
            gpar[s, 0:2] = ret_dl[:, hd]
            r0 = hd * 256
        else:
            q0, k0, v0, o0 = 3072 + hd * 128, 3584 + hd * 128, 4096 + hd * 256, 5120 + hd * 256
            gates = np.zeros((2048, 8), bf)
            for g in range(4):
                gates[:, g] = w_in_b[:, 6144 + g * 4 + hd]
            gpar[s, :] = gate_b[:, hd]
            r0 = 1024 + hd * 256
        w = np.concatenate([w_in_b[:, q0:q0 + 128], w_in_b[:, k0:k0 + 128], w_in_b[:, o0:o0 + 256], w_in_b[:, v0:v0 + 256], gates], axis=1)
        slots.append(w.reshape(KC, 128, 776).transpose(1, 0, 2).reshape(128, KC * 776))
        rows += [r0, r0 + 128]
    win = np.ascontiguousarray(np.stack(slots))
    wo = np.stack([w_out_b[r:r + 128, :] for r in rows])
    wout = np.ascontiguousarray(wo.reshape(8, 128, KC, 128).transpose(2, 1, 0, 3).reshape(KC, 128, 8 * 128))
    gains = np.concatenate([ret_g, ml_g])
    hng = np.ascontiguousarray(np.stack([gains[r:r + 128] for r in rows], axis=1)).astype(np.float32)
    gp = np.ascontiguousarray(np.broadcast_to(gpar.reshape(1, 16), (128, 16))).astype(np.float32)
    return {"win": win, "wout": wout, "hng": hng, "gpar": gp}


def cda_weights(w_in_b, conv_w, conv_b, wa_b, ba, wx_b, bx, lam, hg):
    wl = []; wgt = []; lp = np.zeros((128, 4, 12), np.float32)
    for blk in range(4):
        gb = 4 * hg + blk
        ch0 = gb * 128
        w = np.concatenate([w_in_b[:, ch0:ch0 + 128], w_in_b[:, 1024 + ch0:1024 + ch0 + 128]], axis=1)
        wl.append(w.reshape(KC, 128, 256).transpose(1, 0, 2).reshape(128, KC * 256))
        wgt.append(np.stack([wa_b[0, gb], wx_b[0, gb], wa_b[1, gb], wx_b[1, gb]], axis=1).reshape(128, 4 * 128))
        sl = slice(ch0, ch0 + 128)
        lp[:, blk, 0:4] = conv_w[:, sl].T
        lp[:, blk, 4] = conv_b[sl]
        for d in range(2):
            lp[:, blk, 5 + 3 * d] = ba[d, sl]
            lp[:, blk, 6 + 3 * d] = bx[d, sl]
            lp[:, blk, 7 + 3 * d] = lam[d, sl]
    return {"wlru": np.ascontiguousarray(np.stack(wl)), "wgate": np.ascontiguousarray(np.stack(wgt)), "lpar": np.ascontiguousarray(lp.reshape(128, 48))}


def cdb1_weights(w_in_b, w_uv_b, q_norm_g, kv_norm_g):
    w = w_in_b[:, 2048:2880]
    wmla = np.ascontiguousarray(w.reshape(KC, 128, 832).transpose(1, 0, 2).reshape(128, KC * 832))
    wuv = np.ascontiguousarray(w_uv_b.reshape(2, 128, 1024).transpose(1, 0, 2).reshape(128, 2048))
    npar = np.ascontiguousarray(np.concatenate([tile_vec(q_norm_g), tile_vec(kv_norm_g)], axis=1)).astype(np.float32)
    return {"wmla": wmla, "wuv": wuv, "npar": npar}


def mla_rope_consts():
    half = 32
    freqs = (10000.0 ** (-np.arange(16, dtype=np.float32) / 16)).astype(np.float32)
    t = np.arange(4096)
    row = (t // 64).astype(np.float32); col = (t % 64).astype(np.float32)
    ang_r = row[:, None] * freqs[None, :]; ang_c = col[:, None] * freqs[None, :]
    C = np.zeros((64, 4096), np.float32); S = np.zeros((64, 4096), np.float32)
    perm = np.zeros((64, 64), np.float32)
    for d in range(64):
        grp = d // 16
        ang = ang_r if grp < 2 else ang_c
        f = d % 16
        C[d] = np.cos(ang[:, f])
        s = np.sin(ang[:, f])
        if grp % 2 == 0:
            S[d] = -s; partner = d + 16
        else:
            S[d] = s; partner = d - 16
        perm[partner, d] = 1.0
    return perm, C, S


def cdb2_weights(w_uq_b, w_uk_b, qk_norm_g, w_out_b, hg):
    heads = [4 * hg + h for h in range(4)]
    wq = np.concatenate([w_uq_b[:, h * 192:(h + 1) * 192] for h in heads], axis=1)
    wuq = np.ascontiguousarray(wq.reshape(4, 128, 768).transpose(1, 0, 2).reshape(128, 4 * 768))
    wk = np.concatenate([w_uk_b[:, h * 128:(h + 1) * 128] for h in heads], axis=1)
    wuk = np.ascontiguousarray(wk.reshape(2, 128, 512).transpose(1, 0, 2).reshape(128, 2 * 512))
    mpar = np.zeros((128, 4), np.float32)
    mpar[:, 0] = qk_norm_g[0, :128]; mpar[:64, 1] = qk_norm_g[0, 128:]
    mpar[:, 2] = qk_norm_g[1, :128]; mpar[:64, 3] = qk_norm_g[1, 128:]
    rows = [(4 * hg + blk) * 128 for blk in range(4)] + [1024 + h * 128 for h in heads]
    wo = np.stack([w_out_b[r:r + 128, :] for r in rows])
    wout = np.ascontiguousarray(wo.reshape(8, 128, KC, 128).transpose(2, 1, 0, 3).reshape(KC, 128, 8 * 128))
    return {"wuq": wuq, "wuk": wuk, "mpar": mpar, "wout": wout}


def split_seq(arr_ctx_lat, half):
    return np.ascontiguousarray(np.concatenate([arr_ctx_lat[..., half * 128:(half + 1) * 128], arr_ctx_lat[..., 256 + half * 2048:256 + (half + 1) * 2048]], axis=-1))


def join_seq(a0, a1):
    return np.ascontiguousarray(np.concatenate([a0[..., :128], a1[..., :128], a0[..., 128:], a1[..., 128:]], axis=-1))


_NC_CACHE = {}
_DBG = None


def _dbg(name, val):
    if _DBG is not None:
        _DBG[name] = val


def _get(name, fn, *a, **kw):
    key = (name,) + tuple(map(str, a)) + tuple(sorted((k, str(v)) for k, v in kw.items()))
    if key not in _NC_CACHE:
        _NC_CACHE[key] = fn(*a, **kw)
    return _NC_CACHE[key]


def _run(nc, ims):
    res = run_bass_kernel_spmd(nc, ims, core_ids=list(range(8)))
    return res.results


CAST_CHUNK = 4096
FFN_BLOCKS = [[(0, 512, 0), (512, 256, 0)], [(768, 512, 0), (1280, 256, 0)], [(1536, 512, 0), (2048, 128, 1)]]
FFN_BLOCKS_LAT = [[(0, 512, 0), (512, 256, 0)], [(768, 512, 0), (1280, 256, 0)], [(1536, 512, 0)]]


def _tile_w_in(w):
    return w.reshape(KC, 128, FC, 128).transpose(2, 1, 0, 3).reshape(FC, 128, KC * 128)


def _ffn_weights(wg_b, wu_b, wd_b):
    wgu = np.ascontiguousarray(np.concatenate([_tile_w_in(wg_b), _tile_w_in(wu_b)], axis=2))
    wd = np.ascontiguousarray(wd_b.reshape(FC, 128, KC, 128).transpose(2, 1, 0, 3).reshape(KC, 128, FC * 128))
    return wgu, wd


def _modv(ada_l, b, idx):
    cols = []
    for row in (b, 4):
        for i in idx:
            cols.append(tile_vec(ada_l[row, i]) if i is not None else np.zeros((128, KC), np.float32))
    return np.ascontiguousarray(np.concatenate(cols, axis=1)).astype(np.float32)


def kernel(x, c, ctx, c_ctx, ada_w, ada_b, norm_g, ffn_wg, ffn_wu, ffn_wd, ab_w_in, ab_w_out,
           ret_decay_logit, ret_gn_g, mlstm_gate_b, mlstm_gn_g, cd_w_in, cd_w_out, lru_conv_w,
           lru_conv_b, lru_wa, lru_ba, lru_wx, lru_bx, lru_lambda, mla_q_norm_g, mla_kv_norm_g,
           mla_w_uq, mla_w_uk, mla_w_uv, mla_qk_norm_g):
    f32 = lambda a: np.ascontiguousarray(np.asarray(a, dtype=np.float32))
    x, c, ctx, c_ctx = f32(x), f32(c), f32(ctx), f32(c_ctx)
    B = 4
    big = [f32(ffn_wg), f32(ffn_wu), f32(ffn_wd), f32(ab_w_in), f32(ab_w_out), f32(cd_w_in), f32(cd_w_out),
           f32(mla_w_uq), f32(mla_w_uk), f32(mla_w_uv), f32(lru_wa), f32(lru_wx)]
    sizes = [a.size for a in big]
    tot = sum(sizes)
    unit = 8 * 128 * CAST_CHUNK
    totp = ((tot + unit - 1) // unit) * unit
    flat = np.zeros(totp, np.float32)
    o = 0
    for a in big:
        flat[o:o + a.size] = a.reshape(-1)
        o += a.size
    ncols = totp // (8 * 128)
    flat = flat.reshape(8, 128, ncols)
    cond = np.concatenate([c, c_ctx[None, :]], axis=0)
    condT = np.ascontiguousarray(cond.T.reshape(KC, 128, 5).transpose(1, 0, 2).reshape(128, KC * 5))
    ada_w = f32(ada_w); ada_b = f32(ada_b)
    ims = []
    for k in range(8):
        sl = slice(k * ADA_N, (k + 1) * ADA_N)
        ims.append({"wf32": flat[k], "condT": condT, "adaw": np.ascontiguousarray(ada_w[:, :, sl]),
                    "adab": np.ascontiguousarray(np.broadcast_to(ada_b[:, None, sl], (2, 5, ADA_N)))})
    r = _run(_get("prep", build_prep, ncols, CAST_CHUNK), ims)
    del flat
    wb = np.concatenate([r[k]["wbf16"].reshape(-1) for k in range(8)])
    ada = np.concatenate([r[k]["adao"] for k in range(8)], axis=2).reshape(2, 5, 9, D)
    _dbg("ada", ada)
    outw = []
    o = 0
    for a in big:
        outw.append(wb[o:o + a.size].reshape(a.shape))
        o += a.size
    (wg_b, wu_b, wd_b, abin_b, about_b, cdin_b, cdout_b, uq_b, uk_b, uv_b, wa_b, wx_b) = outw
    norm_g = f32(norm_g)

    def to_ffn_layout(xl, xc):
        outs = []
        for k in range(8):
            b, hf = k // 2, k % 2
            outs.append(np.ascontiguousarray(np.concatenate([xl[b, hf * 2048:(hf + 1) * 2048].T, xc[b, hf * 128:(hf + 1) * 128].T], axis=1)))
        return outs

    def ffn_to_seq(cur, b):
        a0, a1 = cur[2 * b], cur[2 * b + 1]
        return np.ascontiguousarray(np.concatenate([a0[:, 2048:], a1[:, 2048:], a0[:, :2048], a1[:, :2048]], axis=1))

    def seq_to_ffn(seq, hf):
        return np.ascontiguousarray(np.concatenate([seq[:, 256 + hf * 2048:256 + (hf + 1) * 2048], seq[:, hf * 128:(hf + 1) * 128]], axis=1))

    cur = to_ffn_layout(x, ctx)
    cst_ab, ropec, ropes = ab_consts()
    perm, mC, mS = mla_rope_consts()

    for l in range(2):
        last = l == 1
        wgu, wd = _ffn_weights(wg_b[l, 0], wu_b[l, 0], wd_b[l, 0])
        ims = [{"xT": cur[k], "modv": _modv(ada[l], k // 2, (0, 1, 2)), "gn": tile_vec(norm_g[l, 0]), "wgu": wgu, "wd": wd} for k in range(8)]
        r = _run(_get("ffn", build_ffn, 2176, FFN_BLOCKS), ims)
        cur = [r[k]["yT"] for k in range(8)]
        _dbg(f"ffn1_{l}", cur)
        del wgu, wd
        seqs = [ffn_to_seq(cur, b) for b in range(B)]
        mixmod = [_modv(ada[l], b, (3, 4, None)) for b in range(B)]
        gn1 = tile_vec(norm_g[l, 1])
        if l == 0:
            wsets = [ab_weights(abin_b[0], about_b[0], f32(ret_decay_logit)[0], f32(ret_gn_g)[0], f32(mlstm_gate_b)[0], f32(mlstm_gn_g)[0], hg) for hg in range(2)]
            ims = []
            for k in range(8):
                b, hg = k // 2, k % 2
                d = {"xT": seqs[b], "modv": mixmod[b], "gn": gn1, "cst": cst_ab, "ropec": ropec, "ropes": ropes}
                d.update(wsets[hg])
                ims.append(d)
            r = _run(_get("ab", build_ab), ims)
            ypart = [r[k]["yT"] for k in range(8)]
            _dbg("ab", ypart)
            y0 = [seq_to_ffn(ypart[2 * (k // 2)], k % 2) for k in range(8)]
            y1 = [seq_to_ffn(ypart[2 * (k // 2) + 1], k % 2) for k in range(8)]
            T2, blocks2 = 2176, FFN_BLOCKS
        else:
            wsets = [cda_weights(cdin_b[0], f32(lru_conv_w)[0], f32(lru_conv_b)[0], wa_b[0], f32(lru_ba)[0], wx_b[0], f32(lru_bx)[0], f32(lru_lambda)[0], hg) for hg in range(2)]
            ims = []
            for k in range(8):
                b, hg = k // 2, k % 2
                d = {"xT": seqs[b], "modv": mixmod[b], "gn": gn1}
                d.update(wsets[hg])
                ims.append(d)
            ra = _run(_get("cda", build_cda), ims)
            w1 = cdb1_weights(cdin_b[0], uv_b[0], f32(mla_q_norm_g)[0], f32(mla_kv_norm_g)[0])
            ims = []
            for k in range(8):
                d = {"hT": split_seq(ra[2 * (k // 2)]["hT"], k % 2)}
                d.update(w1)
                ims.append(d)
            rb1 = _run(_get("cdb1", build_cdb1), ims)
            w2 = [cdb2_weights(uq_b[0], uk_b[0], f32(mla_qk_norm_g)[0], cdout_b[0], hg) for hg in range(2)]
            ims = []
            for b in range(B):
                r0, r1 = rb1[2 * b], rb1[2 * b + 1]
                ckvn = join_seq(r0["ckvn"], r1["ckvn"])
                kr = join_seq(r0["kr"], r1["kr"])
                cqn = np.ascontiguousarray(np.concatenate([r0["cqn"], r1["cqn"]], axis=-1))
                vfull = np.concatenate([r0["v"][0:1], r1["v"][0:1], r0["v"][1:], r1["v"][1:]], axis=0)
                for hg in range(2):
                    d = {"ckvn": ckvn, "cqn": cqn, "kr": kr, "v": np.ascontiguousarray(vfull[:, :, hg * 512:(hg + 1) * 512]),
                         "mcst": perm, "mrc": mC, "mrs": mS, "mlru": ra[2 * b + hg]["mlru"]}
                    d.update(w2[hg])
                    ims.append(d)
            r = _run(_get("cdb2", build_cdb2), ims)
            ypart = [r[k]["yT"] for k in range(8)]
            _dbg("cd", ypart)
            y0 = [np.ascontiguousarray(ypart[2 * (k // 2)][:, (k % 2) * 2048:(k % 2 + 1) * 2048]) for k in range(8)]
            y1 = [np.ascontiguousarray(ypart[2 * (k // 2) + 1][:, (k % 2) * 2048:(k % 2 + 1) * 2048]) for k in range(8)]
            cur = [np.ascontiguousarray(cur[k][:, :2048]) for k in range(8)]
            T2, blocks2 = 2048, FFN_BLOCKS_LAT
        wgu, wd = _ffn_weights(wg_b[l, 1], wu_b[l, 1], wd_b[l, 1])
        ims = []
        for k in range(8):
            b = k // 2
            g5 = np.ascontiguousarray(np.concatenate([tile_vec(ada[l, b, 5]), tile_vec(ada[l, 4, 5])], axis=1)).astype(np.float32)
            ims.append({"xT": cur[k], "y0T": y0[k], "y1T": y1[k], "g5": g5, "modv": _modv(ada[l], b, (6, 7, 8)),
                        "gn": tile_vec(norm_g[l, 2]), "wgu": wgu, "wd": wd})
        r = _run(_get("ffn", build_ffn, T2, blocks2, combine=True), ims)
        cur = [r[k]["yT"] for k in range(8)]
        _dbg(f"ffn2_{l}", cur)
        del wgu, wd
    out = np.zeros((B, 4096, D), np.float32)
    for k in range(8):
        b, hf = k // 2, k % 2
        out[b, hf * 2048:(hf + 1) * 2048, :] = cur[k].T
    return out
```
